# Optimizing a Trainium2 kernel written in Bass

```python
import math
import jax, jax.numpy as jnp
from jax import lax
import numpy as np

D_MODEL = 1024
BATCH = 8
SEQ = 4096
DEPTH = 1

CHUNK = 64
Q_BLOCK = 128
A_HEADS = 8
A_HEAD_DIM = 64
A_WIDTH = A_HEADS * A_HEAD_DIM
DECAY_LORA = 64
AAA_LORA = 64
GATE_LORA = 128
A_IN = 3 * A_WIDTH + DECAY_LORA + AAA_LORA + GATE_LORA
B_HEADS = 4
B_HEAD_DIM = 64
B_WIDTH = B_HEADS * 2 * B_HEAD_DIM
B_IN = 3 * B_WIDTH
MIX_WIDTH = A_WIDTH + B_WIDTH
IN_WIDTH = A_IN + B_IN
D_FF = 2816
CONV_WIDTH = 3
NORM_EPS = 1e-6
LNX_EPS = 64e-5
DECAY_SCALE = math.exp(-0.5)
L2_EPS = 1e-12

kernel_name = "hymba_rwkv7_diffattn_convffn_block"


def _rms_norm(x, w, eps=NORM_EPS):
    xf = x.astype(jnp.float32)
    y = xf * lax.rsqrt(jnp.mean(xf * xf, axis=-1, keepdims=True) + eps)
    return (y * w.astype(jnp.float32)).astype(x.dtype)


def _token_shift(h):
    return jnp.pad(h, ((0, 0), (1, 0), (0, 0)))[:, :-1]


def _rwkv7_scan(r, w, k, v, a, b):
    bsz, _, nh, n = r.shape

    def step(S, inp):
        r_t, w_t, k_t, v_t, a_t, b_t = inp
        sa = jnp.einsum('bhvk,bhk->bhv', S, a_t)
        S = (S * w_t[:, :, None, :] + sa[..., None] * b_t[:, :, None, :]
             + v_t[..., None] * k_t[:, :, None, :])
        return S, jnp.einsum('bhvk,bhk->bhv', S, r_t)

    seq = (jnp.moveaxis(r, 1, 0), jnp.moveaxis(w, 1, 0), jnp.moveaxis(k, 1, 0),
           jnp.moveaxis(v, 1, 0), jnp.moveaxis(a, 1, 0), jnp.moveaxis(b, 1, 0))
    S0 = jnp.zeros((bsz, nh, n, n), jnp.float32)
    _, y = lax.scan(step, S0, seq)
    return jnp.moveaxis(y, 0, 1)


def _rwkv7_mixer(h, mu, w0, w_decay_up, a0, w_aaa_up, w_gate_up, k_k, k_a, r_k, ln_x_w, ln_x_b):
    bsz, T, _ = h.shape
    h = h + (_token_shift(h) - h) * mu
    s1, s2, s3 = A_WIDTH, 2 * A_WIDTH, 3 * A_WIDTH
    r, k, v, wd, ad, gd = jnp.split(h, [s1, s2, s3, s3 + DECAY_LORA, s3 + DECAY_LORA + AAA_LORA], axis=-1)
    w = jnp.exp(-DECAY_SCALE * jax.nn.sigmoid((w0 + jnp.tanh(wd) @ w_decay_up).astype(jnp.float32)))
    a = jax.nn.sigmoid(a0 + ad @ w_aaa_up)
    g = jax.nn.sigmoid(gd) @ w_gate_up

    def heads(t):
        return t.reshape(bsz, T, A_HEADS, A_HEAD_DIM).astype(jnp.float32)

    kk = heads(k * k_k)
    kk = kk * lax.rsqrt(jnp.sum(kk * kk, axis=-1, keepdims=True) + L2_EPS)
    k_h = heads(k * (1.0 + (a - 1.0) * k_a))
    r_h, v_h, a_h, w_h = heads(r), heads(v), heads(a), heads(w)
    y = _rwkv7_scan(r_h, w_h, k_h, v_h, -kk, kk * a_h)
    mean = jnp.mean(y, axis=-1, keepdims=True)
    var = jnp.mean(jnp.square(y - mean), axis=-1, keepdims=True)
    y = ((y - mean) * lax.rsqrt(var + LNX_EPS)).reshape(bsz, T, A_WIDTH)
    y = y * ln_x_w.astype(jnp.float32) + ln_x_b.astype(jnp.float32)
    bonus = jnp.sum(r_h * k_h * r_k.astype(jnp.float32), axis=-1, keepdims=True) * v_h
    y = y + bonus.reshape(bsz, T, A_WIDTH)
    return (y * g.astype(jnp.float32)).astype(h.dtype)


def _diff_attention(h, q_norm_w, k_norm_w, lambda_q1, lambda_k1, lambda_q2, lambda_k2, subln_w, lambda_init):
    bsz, T, _ = h.shape
    nb = T // Q_BLOCK
    q, k, v = jnp.split(h, 3, axis=-1)
    q = _rms_norm(q.reshape(bsz, T, B_HEADS, 2, B_HEAD_DIM), q_norm_w)
    k = _rms_norm(k.reshape(bsz, T, B_HEADS, 2, B_HEAD_DIM), k_norm_w)
    v = v.reshape(bsz, T, B_HEADS, 2 * B_HEAD_DIM)
    lam = (jnp.exp(jnp.sum(lambda_q1 * lambda_k1).astype(jnp.float32))
           - jnp.exp(jnp.sum(lambda_q2 * lambda_k2).astype(jnp.float32)) + lambda_init)
    slopes = jnp.exp2(-8.0 * jnp.arange(1, B_HEADS + 1, dtype=jnp.float32) / B_HEADS)
    scale = 1.0 / math.sqrt(B_HEAD_DIM)
    q_blocks = q.reshape(bsz, nb, Q_BLOCK, B_HEADS, 2, B_HEAD_DIM).transpose(1, 0, 3, 4, 2, 5)
    k_t = k.transpose(0, 2, 3, 1, 4)
    v_t = v.transpose(0, 2, 1, 3)
    key_pos = jnp.arange(T)

    def block(args):
        i, q_blk = args
        q_pos = i * Q_BLOCK + jnp.arange(Q_BLOCK)
        s = jnp.einsum('bhmqd,bhmkd->bhmqk', q_blk, k_t).astype(jnp.float32) * scale
        dist = jnp.abs(q_pos[:, None] - key_pos[None, :]).astype(jnp.float32)
        bias = -slopes[:, None, None, None] * dist[None, None]
        allowed = (key_pos[None, :] // CHUNK) <= (q_pos[:, None] // CHUNK)
        s = jnp.where(allowed, s + bias, -jnp.inf)
        p = jax.nn.softmax(s, axis=-1)
        attn = p[:, :, 0] - lam * p[:, :, 1]
        return jnp.einsum('bhqk,bhkd->bhqd', attn.astype(v_t.dtype), v_t)

    o = lax.map(block, (jnp.arange(nb), q_blocks))
    o = o.transpose(1, 0, 3, 2, 4).reshape(bsz, T, B_HEADS, 2 * B_HEAD_DIM)
    o = _rms_norm(o, subln_w) * (1.0 - lambda_init)
    return o.reshape(bsz, T, B_WIDTH)


def _conv_ffn(h, w_up, conv_w, conv_b, w_down):
    u = h @ w_up
    u = lax.conv_general_dilated(
        u, conv_w[:, None, :].astype(u.dtype), window_strides=(1,),
        padding=[(CONV_WIDTH - 1, 0)], dimension_numbers=('NWC', 'WIO', 'NWC'),
        feature_group_count=2 * D_FF) + conv_b
    gate, up = jnp.split(u, 2, axis=-1)
    return (jax.nn.silu(gate) * up) @ w_down


def setup_inputs(seed: int = 0) -> dict:
    key = jax.random.key(seed)
    ks = jax.random.split(key, 32)
    L = DEPTH

    def nrm(k, shape, s):
        return s * jax.random.normal(k, shape, jnp.float32)

    return {
        "x": nrm(ks[0], (BATCH, SEQ, D_MODEL), 1.0),
        "attn_norm_w": 1.0 + nrm(ks[1], (L, D_MODEL), 0.02),
        "w_in": nrm(ks[2], (L, D_MODEL, IN_WIDTH), D_MODEL ** -0.5),
        "mu_shift": jax.random.uniform(ks[3], (L, A_IN), jnp.float32),
        "w0": nrm(ks[4], (L, A_WIDTH), 1.0),
        "w_decay_up": nrm(ks[5], (L, DECAY_LORA, A_WIDTH), 0.5 * DECAY_LORA ** -0.5),
        "a0": nrm(ks[6], (L, A_WIDTH), 0.5),
        "w_aaa_up": nrm(ks[7], (L, AAA_LORA, A_WIDTH), 0.5 * AAA_LORA ** -0.5),
        "w_gate_up": nrm(ks[8], (L, GATE_LORA, A_WIDTH), GATE_LORA ** -0.5),
        "k_k": 0.85 + nrm(ks[9], (L, A_WIDTH), 0.02),
        "k_a": 1.0 + nrm(ks[10], (L, A_WIDTH), 0.02),
        "r_k": nrm(ks[11], (L, A_HEADS, A_HEAD_DIM), 0.1),
        "ln_x_w": 1.0 + nrm(ks[12], (L, A_WIDTH), 0.02),
        "ln_x_b": nrm(ks[13], (L, A_WIDTH), 0.02),
        "q_norm_w": 1.0 + nrm(ks[14], (L, B_HEAD_DIM), 0.02),
        "k_norm_w": 1.0 + nrm(ks[15], (L, B_HEAD_DIM), 0.02),
        "lambda_q1": nrm(ks[16], (L, B_HEAD_DIM), 0.1),
        "lambda_k1": nrm(ks[17], (L, B_HEAD_DIM), 0.1),
        "lambda_q2": nrm(ks[18], (L, B_HEAD_DIM), 0.1),
        "lambda_k2": nrm(ks[19], (L, B_HEAD_DIM), 0.1),
        "subln_w": 1.0 + nrm(ks[20], (L, 2 * B_HEAD_DIM), 0.02),
        "w_out": nrm(ks[21], (L, MIX_WIDTH, D_MODEL), MIX_WIDTH ** -0.5),
        "ffn_norm_w": 1.0 + nrm(ks[22], (L, D_MODEL), 0.02),
        "w_ffn_up": nrm(ks[23], (L, D_MODEL, 2 * D_FF), D_MODEL ** -0.5),
        "ffn_conv_w": nrm(ks[24], (L, CONV_WIDTH, 2 * D_FF), CONV_WIDTH ** -0.5),
        "ffn_conv_b": nrm(ks[25], (L, 2 * D_FF), 0.02),
        "w_ffn_down": nrm(ks[26], (L, D_FF, D_MODEL), D_FF ** -0.5),
    }


def reference(x, attn_norm_w, w_in, mu_shift, w0, w_decay_up, a0, w_aaa_up, w_gate_up,
              k_k, k_a, r_k, ln_x_w, ln_x_b, q_norm_w, k_norm_w, lambda_q1, lambda_k1,
              lambda_q2, lambda_k2, subln_w, w_out, ffn_norm_w, w_ffn_up, ffn_conv_w,
              ffn_conv_b, w_ffn_down):
    for l in range(DEPTH):
        lambda_init = 0.8 - 0.6 * math.exp(-0.3 * l)
        h = _rms_norm(x, attn_norm_w[l])
        z = h @ w_in[l]
        y_a = _rwkv7_mixer(z[..., :A_IN], mu_shift[l], w0[l], w_decay_up[l], a0[l], w_aaa_up[l],
                           w_gate_up[l], k_k[l], k_a[l], r_k[l], ln_x_w[l], ln_x_b[l])
        y_b = _diff_attention(z[..., A_IN:], q_norm_w[l], k_norm_w[l], lambda_q1[l], lambda_k1[l],
                              lambda_q2[l], lambda_k2[l], subln_w[l], lambda_init)
        x = x + jnp.concatenate([y_a, y_b], axis=-1) @ w_out[l]
        x = x + _conv_ffn(_rms_norm(x, ffn_norm_w[l]), w_ffn_up[l], ffn_conv_w[l],
                          ffn_conv_b[l], w_ffn_down[l])
    return x
```

```python
import numpy as np
import concourse.bass as bass
import concourse.mybir as mybir
from concourse.bass_utils import run_bass_kernel_spmd

F32 = mybir.dt.float32
BF16 = mybir.dt.bfloat16
ALU = mybir.AluOpType
AF = mybir.ActivationFunctionType
AX = mybir.AxisListType

ENGS = ("pe", "act", "dve", "pool", "sp")


class Op:
    __slots__ = ("eng", "seq", "fn", "waits", "signal", "is_dma", "slot", "slotval", "semval", "gid")

    def __init__(self, eng, seq, fn, is_dma):
        self.eng = eng
        self.seq = seq
        self.fn = fn
        self.waits = []
        self.signal = False
        self.is_dma = is_dma
        self.slot = None
        self.slotval = None
        self.semval = None


def _footprint(ap):
    t = ap.tensor
    name = t.name
    space = str(ap.space)
    dims = ap.ap
    off = int(ap.offset)
    isz = mybir.dt.size(ap.dtype)
    if space == "PSUM":
        return name, True, (0, 128, 0, 1 << 30)
    if space == "SB":
        psz = dims[0][0]
        if psz <= 0:
            psz = 1
            for s in t.shape[1:]:
                psz *= int(s)
        p0 = off // psz
        f0 = off % psz
        pc = dims[0][1]
        ext = 1
        for st, cnt in dims[1:]:
            ext += (cnt - 1) * abs(st)
        return name, False, (p0, p0 + pc, f0 * isz, (f0 + ext) * isz)
    ext = 1
    for st, cnt in dims:
        ext += (cnt - 1) * abs(st)
    return name, False, (0, 1, off * isz, (off + ext) * isz)


class Sched:
    def __init__(self, n_slots=40):
        self.ops = {e: [] for e in ENGS}
        self.known = {e: {} for e in ENGS}
        self.recs = {}
        self.n_slots = n_slots
        self.n_dma = 0
        self.n_dma_q = {}
        self.slot_last = [None] * n_slots
        self.all_dma_out = []

    def _need(self, op, dep):
        if dep is None or dep is op:
            return
        if dep.is_dma:
            key = ("dma", dep.slot)
            val = dep.slotval
        else:
            if dep.eng == op.eng and op.eng == "pe":
                return
            key = dep.eng
            val = dep.seq
        kn = self.known[op.eng]
        if kn.get(key, -1) >= val:
            return
        kn[key] = val
        dep.signal = True
        for i, (k, d) in enumerate(op.waits):
            if k == key:
                op.waits[i] = (key, dep)
                return
        op.waits.append((key, dep))

    def _access(self, op, ap, is_write):
        name, is_psum, fp = _footprint(ap)
        recs = self.recs.setdefault(name, [])
        p0, p1, f0, f1 = fp
        keep = []
        for r in recs:
            rp0, rp1, rf0, rf1, wr, rds = r
            if rp0 < p1 and p0 < rp1 and rf0 < f1 and f0 < rf1:
                if wr is not None:
                    self._need(op, wr)
                if is_write or is_psum:
                    for rd in rds:
                        self._need(op, rd)
                if is_write and rp0 >= p0 and rp1 <= p1 and rf0 >= f0 and rf1 <= f1:
                    continue
            keep.append(r)
        recs[:] = keep
        if is_write:
            recs.append([p0, p1, f0, f1, op, []])
        else:
            for r in recs:
                if r[0] == p0 and r[1] == p1 and r[2] == f0 and r[3] == f1:
                    rds = r[5]
                    if not op.is_dma:
                        rds[:] = [x for x in rds if x.eng != op.eng or x.is_dma]
                    rds.append(op)
                    return
            recs.append([p0, p1, f0, f1, None, [op]])

    def add(self, eng, fn, reads=(), writes=(), is_dma=False):
        op = Op(eng, len(self.ops[eng]), fn, is_dma)
        if is_dma:
            n = self.n_dma_q.get(eng, 0)
            self.n_dma_q[eng] = n + 1
            self.n_dma += 1
            if eng == "pool":
                base, cnt = self.n_slots - 12, 12
            else:
                base, cnt = 0, self.n_slots - 12
            op.slot = base + n % cnt
            op.slotval = 16 * (n // cnt + 1)
            prev = self.slot_last[op.slot]
            if prev is not None:
                self._need(op, prev)
            self.slot_last[op.slot] = op
        for ap in reads:
            self._access(op, ap, False)
        for ap in writes:
            self._access(op, ap, True)
        self.ops[eng].append(op)
        return op

    def dma(self, out, in_, eng="sp", is_output=False):
        op = self.add(eng, lambda e: e.dma_start(out=out, in_=in_), [in_], [out], is_dma=True)
        if is_output:
            self.all_dma_out.append(op)
        return op

    def mm(self, out, lhsT, rhs, start=True, stop=True, skip=False):
        return self.add("pe", lambda e: e.matmul(out, lhsT, rhs, start=start, stop=stop,
                                                 skip_group_check=skip), [lhsT, rhs], [out])

    def tr(self, out, in_, ident):
        return self.add("pe", lambda e: e.transpose(out, in_, ident), [in_, ident], [out])

    def act(self, out, in_, func, bias=0.0, scale=1.0, accum=None, eng="act"):
        rd = [in_]
        if not isinstance(bias, (int, float)):
            rd.append(bias)
        if not isinstance(scale, (int, float)):
            rd.append(scale)
        wr = [out]
        if accum is not None:
            wr.append(accum)

        def fn(e):
            if accum is not None:
                return e.activation(out, in_, func, bias=bias, scale=scale, accum_out=accum)
            return e.activation(out, in_, func, bias=bias, scale=scale)
        return self.add("act", fn, rd, wr)

    def tt(self, eng, out, in0, in1, op):
        return self.add(eng, lambda e: e.tensor_tensor(out, in0, in1, op), [in0, in1], [out])

    def ts(self, eng, out, in0, s1, s2, op0, op1=None, accum=None):
        rd = [in0]
        if not isinstance(s1, (int, float)):
            rd.append(s1)
        if s2 is not None and not isinstance(s2, (int, float)):
            rd.append(s2)
        wr = [out]
        if accum is not None:
            wr.append(accum)

        def fn(e):
            kw = {}
            if accum is not None:
                kw["accum_out"] = accum
            if op1 is None:
                return e.tensor_scalar(out, in0, s1, None, op0, **kw)
            return e.tensor_scalar(out, in0, s1, s2, op0, op1, **kw)
        return self.add(eng, fn, rd, wr)

    def stt(self, out, in0, scalar, in1, op0, op1, eng="dve"):
        rd = [in0, in1]
        if not isinstance(scalar, (int, float)):
            rd.append(scalar)
        return self.add(eng, lambda e: e.scalar_tensor_tensor(out, in0, scalar, in1, op0, op1), rd, [out])

    def copy(self, eng, out, in_):
        if eng == "act":
            return self.add("act", lambda e: e.copy(out, in_), [in_], [out])
        return self.add(eng, lambda e: e.tensor_copy(out, in_), [in_], [out])

    def memset(self, eng, ap, val):
        return self.add(eng, lambda e: e.memset(ap, val), [], [ap])

    def rsqrt(self, out, in_):
        self.act(out, in_, AF.Ln)
        return self.act(out, out, AF.Exp, scale=-0.5)

    def recip(self, out, in_):
        return self.add("dve", lambda e: e.reciprocal(out, in_), [in_], [out])

    def scan(self, out, d0, d1, init, op0, op1):
        rd = [d0, d1]
        if not isinstance(init, (int, float)):
            rd.append(init)
        return self.add("dve", lambda e: e.tensor_tensor_scan(out, d0, d1, init, op0, op1), rd, [out])

    def reduce(self, out, in_, op, axis=AX.X, eng="dve"):
        return self.add(eng, lambda e: e.tensor_reduce(out, in_, axis, op), [in_], [out])

    def emit(self, nc, block, sems, dsems):
        for e in ENGS:
            c = 0
            for op in self.ops[e]:
                if op.signal and not op.is_dma:
                    c += 1
                    op.semval = c
        engobj = {"pe": "tensor", "act": "scalar", "dve": "vector", "pool": "gpsimd", "sp": "sync"}
        outs = self.all_dma_out

        def make(ename):
            ops = self.ops[ename]

            def body(eng):
                for op in ops:
                    for key, dep in op.waits:
                        if dep.is_dma:
                            eng.wait_ge(dsems[dep.slot], dep.slotval)
                        else:
                            eng.wait_ge(sems[dep.eng], dep.semval)
                    ins = op.fn(eng)
                    if op.is_dma:
                        ins.then_inc(dsems[op.slot], 16)
                    elif op.signal:
                        ins.then_inc(sems[op.eng], 1)
                if ename == "sp":
                    for op in outs:
                        eng.wait_ge(dsems[op.slot], op.slotval)
            return body

        for e in ENGS:
            if not self.ops[e] and e != "sp":
                continue
            getattr(block, engobj[e])(make(e))


DEC_SCALE = float(np.exp(-0.5))
SLOPES = [2.0 ** (-8.0 * (h + 1) / 4.0) for h in range(4)]
LAMBDA_INIT = 0.8 - 0.6 * 1.0

PV_ANW, PV_FNW, PV_MU, PV_W0, PV_A0, PV_KK, PV_KA, PV_RK, PV_LNW, PV_LNB, PV_QW, PV_KW = \
    0, 8, 16, 30, 34, 38, 42, 46, 50, 54, 58, 59
PV_CW0, PV_CW1, PV_CW2, PV_CB = 60, 104, 148, 192
NPV = 236
C_BLK1, C_BLKM, C_NH, C_D = 0, 128, 256, 768
NCST = 768 + 2560
CB_ID, CB_SU, CB_SL, CB_UI, CB_I4, CB_RM = 0, 128, 640, 1152, 1664, 2176
CB_KAUG, CB_QAUG = 2688, 3200
NCB = 3712

ARENA_F32 = 52900


class Arena:
    def __init__(self, ar, n):
        self.ar = ar
        self.n = n
        self.p = 0
        self.hi = 0

    def f32(self, n):
        a = self.p
        self.p += n
        assert self.p <= self.n, ("arena overflow", self.p, self.n)
        self.hi = max(self.hi, self.p)
        return self.ar[:, a:a + n]

    def bf(self, n):
        assert n % 2 == 0
        a = self.p
        self.p += n // 2
        assert self.p <= self.n, ("arena overflow", self.p, self.n)
        self.hi = max(self.hi, self.p)
        return self.ar[:, a:a + n // 2].bitcast(BF16)


def build(T, do_a1=True, do_a2=True, do_b=True):
    from contextlib import ExitStack
    NST = T // 512
    NKT = T // 128
    nc = bass.Bass("TRN2", target_bir_lowering=False)

    def din(name, shape):
        return nc.dram_tensor(name, shape, F32, kind="ExternalInput").ap()

    x = din("x", [T, 1024])
    w_in = din("w_in", [1024, 3328])
    w_out = din("w_out", [1024, 1024])
    w_up = din("w_up", [1024, 5632])
    w_dn = din("w_dn", [2816, 1024])
    w_lora = din("w_lora", [128, 512])
    w_gate = din("w_gate", [128, 512])
    pvec = din("pvec", [128, NPV])
    cst = din("cst", [128, NCST])
    cstb = din("cstb", [128, NCB])
    lamv = din("lamv", [128, 256])
    subw = din("subw", [128, 128])
    out = nc.dram_tensor("out", [T, 1024], F32, kind="ExternalOutput").ap()

    S = Sched()
    es = ExitStack()
    with es:
        AR = es.enter_context(nc.sbuf_tensor("AR", [128, ARENA_F32], F32))
        al = Arena(AR, ARENA_F32)
        pbs = [es.enter_context(nc.psum_tensor("pb%d" % i, [128, 512], F32)) for i in range(6)]
        tqs = [es.enter_context(nc.psum_tensor("tq%d" % i, [128, 1024], BF16)) for i in range(2)]
        state = {"pb": 0, "tq": 0}

        def bank():
            b = pbs[state["pb"] % 6]
            state["pb"] += 1
            return b

        def tqhalf():
            k = state["tq"] % 4
            state["tq"] += 1
            return tqs[k // 2][:, (k % 2) * 512:(k % 2) * 512 + 512]

        pv = al.f32(NPV)
        der = al.f32(32)
        omu = der[:, 0:14]
        omka = der[:, 14:18]
        qws = der[:, 18:19]
        kws = der[:, 19:20]
        lam = der[:, 20:21]
        nlam = der[:, 21:22]
        cs = al.f32(768)
        blk1 = cs[:, C_BLK1:C_BLK1 + 128]
        blkm = cs[:, C_BLKM:C_BLKM + 128]
        negh = cs[:, C_NH:C_NH + 512]
        cb = al.bf(NCB)
        ident = cb[:, CB_ID:CB_ID + 128]
        SU4 = cb[:, CB_SU:CB_SU + 512]
        SL4 = cb[:, CB_SL:CB_SL + 512]
        UI4 = cb[:, CB_UI:CB_UI + 512]
        I4 = cb[:, CB_I4:CB_I4 + 512]
        rmask = cb[:, CB_RM:CB_RM + 512]
        kaug = cb[:, CB_KAUG:CB_KAUG + 512].rearrange("p (h k) -> p h k", h=4)
        qaug = cb[:, CB_QAUG:CB_QAUG + 512]
        hT = al.bf(4096).rearrange("p (k t) -> p k t", k=8)
        nrm = al.f32(16)
        NT = {}

        S.dma(pv, pvec)
        S.dma(cs, cst[:, 0:768])
        S.dma(cb, cstb, eng="pool")
        S.ts("dve", omu, pv[:, PV_MU:PV_MU + 14], -1.0, 1.0, ALU.mult, ALU.add)
        S.ts("dve", omka, pv[:, PV_KA:PV_KA + 4], -1.0, 1.0, ALU.mult, ALU.add)
        S.ts("dve", qws, pv[:, PV_QW:PV_QW + 1], 0.125, None, ALU.mult)
        S.ts("dve", kws, pv[:, PV_KW:PV_KW + 1], 1.0, None, ALU.mult)
        base_mark = al.p

        def norm_T_gen(xtile_of_j, wcol0, hT_=None, tqb=None, hn_=None, slack=1, defer=0, hns=None,
                       pool_rsqrt=False):
            hT_ = hT if hT_ is None else hT_
            hns = hns or [NT["hn"] if hn_ is None else hn_]
            junk = NT["junk"]
            tqb = tqb or [tqs[0], tqs[1]]
            wbc = pv[:, wcol0:wcol0 + 8].unsqueeze(2).to_broadcast([128, 8, 128])
            tiles = {0: xtile_of_j(0), 1: xtile_of_j(1)}
            pending = []
            clock = [0]

            def tick():
                clock[0] += 1
                while pending and pending[0][0] <= clock[0]:
                    pending.pop(0)[1]()

            def make_tr(j, hnj):
                def f():
                    tq = tqb[j % len(tqb)]
                    for kc in range(8):
                        S.tr(tq[:, kc * 128:(kc + 1) * 128], hnj[:, kc * 128:(kc + 1) * 128], ident)
                    S.tt("dve", hT_[:, :, j * 128:(j + 1) * 128],
                         tq[:, :].rearrange("p (k t) -> p k t", k=8), wbc, ALU.mult)
                return f
            yield
            yield
            for j in range(4):
                xt = tiles[j]
                hnj = hns[j % len(hns)]
                tick()
                yield
                S.act(junk, xt, AF.Square, accum=nrm[:, j:j + 1])
                tick()
                yield
                S.ts("dve", nrm[:, 4 + j:5 + j], nrm[:, j:j + 1], 1.0 / 1024.0, 1e-6, ALU.mult, ALU.add)
                tick()
                yield
                if pool_rsqrt:
                    S.tt("pool", nrm[:, 8 + j:9 + j], nrm[:, 4 + j:5 + j], negh[:, 0:1], ALU.pow)
                    tick()
                    yield
                else:
                    S.act(nrm[:, 8 + j:9 + j], nrm[:, 4 + j:5 + j], AF.Ln)
                    tick()
                    yield
                    S.act(nrm[:, 8 + j:9 + j], nrm[:, 8 + j:9 + j], AF.Exp, scale=-0.5)
                    tick()
                    yield
                S.ts("dve", hnj, xt, nrm[:, 8 + j:9 + j], None, ALU.mult)
                if j + 2 < 4:
                    tiles[j + 2] = xtile_of_j(j + 2)
                if defer > 0:
                    pending.append((clock[0] + defer, make_tr(j, hnj)))
                    tick()
                    yield
                else:
                    for _ in range(slack):
                        yield
                    make_tr(j, hnj)()
                    yield
            while pending:
                tick()
                yield

        def norm_T(xtile_of_j, wcol0):
            for _ in norm_T_gen(xtile_of_j, wcol0):
                pass

        def proj_fm(wt, col0):
            b = bank()
            for kc in range(8):
                S.mm(b[:], wt[:, kc, col0:col0 + 128], hT[:, kc, :], start=(kc == 0), stop=(kc == 7))
            return b

        pre_a2 = {}
        if do_a1 and do_a2:
            al.p = base_mark
            pre_a2["w2"] = al.bf(8 * 1536).rearrange("p (k n) -> p k n", k=8)
            pre_a2["wob"] = al.bf(4 * 1024).rearrange("p (k n) -> p k n", k=4)
            pre_a2["end"] = al.p
        if do_a1:
            al.p = base_mark
            NT["hn"] = al.bf(1024)
            NT["junk"] = al.bf(1024)
            w1 = al.bf(8 * 1792).rearrange("p (k n) -> p k n", k=8)
            w1_end = al.p
            woa = al.bf(4 * 1024).rearrange("p (k n) -> p k n", k=4)
            wlo = al.bf(512)
            wga = al.bf(512)
            xinA = [al.f32(1024) for _ in range(2)]
            hnA2 = al.bf(1024)
            zraw = [al.f32(514) for _ in range(2)]
            carry2 = [al.f32(16) for _ in range(2)]
            lora12 = al.bf(512)
            lora13 = al.bf(512)
            rT = al.bf(2048).rearrange("p (c t) -> p c t", c=4)
            aT = al.bf(2048).rearrange("p (c t) -> p c t", c=4)
            bT = al.bf(2048).rearrange("p (c t) -> p c t", c=4)
            kT = al.bf(2048).rearrange("p (c t) -> p c t", c=4)
            vT = al.bf(2048).rearrange("p (c t) -> p c t", c=4)
            tokV = al.bf(2048).rearrange("p (j f) -> p j f", j=4)
            tokB = al.bf(2048).rearrange("p (j f) -> p j f", j=4)
            tokK = al.bf(2048).rearrange("p (j f) -> p j f", j=4)
            gsb = al.bf(2048).rearrange("p (c t) -> p c t", c=4)
            bon = al.bf(2048).rearrange("p (c t) -> p c t", c=4)
            ysb = al.f32(2048).rearrange("p (c t) -> p c t", c=4)
            mixA = rT
            CH = [{"M": [al.bf(512) for _ in range(2)], "MT": [al.bf(512) for _ in range(2)],
                   "P": [al.bf(512) for _ in range(2)]} for _ in range(4)]
            AM = [[al.bf(1024) for _ in range(3)] for _ in range(2)]
            Gbf = al.bf(512)
            Ubf = al.bf(512)
            S0T = al.f32(256)
            S0bf = al.bf(256)
            tmpS = al.f32(256)
            Wc = al.f32(16).rearrange("p (c j) -> p c j", c=4)
            tl = [al.f32(512) for _ in range(9)]
            ysb_flat = ysb.rearrange("p c t -> p (c t)")
            tlb = [ysb_flat[:, i * 512:(i + 1) * 512] for i in range(4)] + [al.f32(512) for _ in range(5)]
            TS = [{"tl": tl, "zraw": zraw, "t1": tl[8]},
                  {"tl": tlb, "zraw": [al.f32(514) for _ in range(2)], "t1": tlb[8]}]
            zl = al.f32(512)
            eps12 = al.f32(2)
            S.memset("pool", eps12, 1e-12)
            epsln = al.f32(2)
            S.memset("pool", epsln, 64e-5)
            print("A1 arena words", al.p)

            for kc in range(8):
                S.dma(w1[:, kc, :], w_in[kc * 128:(kc + 1) * 128, 0:1792], eng="pool")
            for kc in range(4):
                S.dma(woa[:, kc, :], w_out[kc * 128:(kc + 1) * 128, :], eng="pool")
            S.dma(wlo, w_lora, eng="pool")
            S.dma(wga, w_gate, eng="pool")
            for c_ in carry2:
                S.memset("pool", c_, 0.0)
            S.memset("pool", S0T, 0.0)
            S.memset("pool", S0bf, 0.0)

            cur_s = [0]

            def lerp_chunk(c, b, zout, ts_=None):
                s_ = cur_s[0]
                cold, cnew = carry2[(s_ + 1) % 2], carry2[s_ % 2]
                mu_ = pv[:, PV_MU + c:PV_MU + c + 1]
                S.act(zout, b[:], AF.Copy, scale=omu[:, c:c + 1])
                S.copy("act", cnew[:, c:c + 1], b[:, 511:512])
                S.stt(zout[:, 1:512], b[:, 0:511], mu_, zout[:, 1:512], ALU.mult, ALU.add)
                S.stt(zout[:, 0:1], cold[:, c:c + 1], mu_, zout[:, 0:1], ALU.mult, ALU.add)

            def gen_normA(s_):
                t0_ = s_ * 512

                def xload(j):
                    xt = xinA[j % 2]
                    S.dma(xt, x[t0_ + j * 128:t0_ + (j + 1) * 128, :])
                    return xt
                yield from norm_T_gen(xload, PV_ANW, defer=6, hns=[NT["hn"], hnA2])

            for _ in gen_normA(0):
                pass
            for s in range(NST):
                t0 = s * 512
                cur_s[0] = s
                bgn = gen_normA(s + 1) if s + 1 < NST else iter(())
                def gen_lora():
                    lerp_chunk(12, proj_fm(w1, 12 * 128), zl)
                    yield
                    S.act(lora12[0:64, :], zl[0:64, :], AF.Tanh)
                    S.copy("pool", lora12[64:128, :], zl[64:128, :])
                    yield
                    lerp_chunk(13, proj_fm(w1, 13 * 128), zl)
                    yield
                    S.act(lora13, zl, AF.Sigmoid)
                    yield
                def gen_pair(c, ts_, delay):
                    for _ in range(delay):
                        yield
                    tl_ = ts_["tl"]
                    zr_, zk_, zv_ = tl_[0], tl_[1], tl_[2]
                    lerp_chunk(c, proj_fm(w1, c * 128), zr_, ts_)
                    yield
                    lerp_chunk(4 + c, proj_fm(w1, (4 + c) * 128), zk_, ts_)
                    yield
                    lerp_chunk(8 + c, proj_fm(w1, (8 + c) * 128), zv_, ts_)
                    yield
                    cc = slice(c * 128, (c + 1) * 128)
                    b = bank()
                    S.mm(b[:], wlo[0:64, cc], lora12[0:64, :])
                    sg = tl_[3]
                    S.act(sg, b[:], AF.Sigmoid, bias=pv[:, PV_W0 + c:PV_W0 + c + 1])
                    b = bank()
                    S.mm(b[:], wlo[64:128, cc], lora12[64:128, :])
                    apm = tl_[8]
                    S.act(apm, b[:], AF.Sigmoid, bias=pv[:, PV_A0 + c:PV_A0 + c + 1])
                    yield
                    Lc = tl_[4]
                    S.scan(Lc, rmask, sg, 0.0, ALU.mult, ALU.add)
                    b = bank()
                    S.mm(b[:], wga[:, cc], lora13)
                    S.copy("act", gsb[:, c, :], b[:])
                    kk = tl_[6]
                    S.ts("dve", kk, zk_, pv[:, PV_KK + c:PV_KK + c + 1], None, ALU.mult)
                    kk2 = tl_[7]
                    S.act(kk2, zk_, AF.Square, scale=pv[:, PV_KK + c:PV_KK + c + 1])
                    b = bank()
                    S.mm(b[:], blk1, kk2)
                    yield
                    eL, enL = tl_[3], tl_[5]
                    S.act(eL, Lc, AF.Exp, scale=-DEC_SCALE)
                    S.act(enL, Lc, AF.Exp, scale=DEC_SCALE)
                    S.copy("pool", Wc[:, c, :], eL[:, 127:512:128])
                    S.act(kk2, b[:], AF.Ln, bias=eps12[:, 0:1])
                    S.act(kk2, kk2, AF.Exp, scale=-0.5)
                    yield
                    t4 = tl_[4]
                    S.ts("dve", t4, apm, pv[:, PV_KA + c:PV_KA + c + 1], omka[:, c:c + 1], ALU.mult, ALU.add)
                    S.tt("pool", t4, zk_, t4, ALU.mult)
                    S.tt("pool", apm, apm, enL, ALU.mult)
                    yield
                    S.tt("dve", kT[:, c, :], t4, enL, ALU.mult)
                    S.tt("dve", rT[:, c, :], zr_, eL, ALU.mult)
                    rk = tl_[1]
                    S.stt(rk, zr_, pv[:, PV_RK + c:PV_RK + c + 1], t4, ALU.mult, ALU.mult)
                    bb_ = bank()
                    S.mm(bb_[:], blk1, rk)
                    S.copy("act", vT[:, c, :], zv_)
                    yield
                    S.tt("dve", kk, kk, kk2, ALU.mult)
                    S.stt(aT[:, c, 1:512], kk[:, 1:512], -1.0, eL[:, 0:511], ALU.mult, ALU.mult)
                    S.ts("dve", aT[:, c, 0:512:128], kk[:, 0:512:128], -1.0, None, ALU.mult)
                    S.tt("dve", bT[:, c, :], kk, apm, ALU.mult)
                    yield
                    S.tt("dve", bon[:, c, :], bb_[:], zv_, ALU.mult)
                    yield

                def seq(*gs):
                    for g in gs:
                        yield from g

                def rr(gens):
                    gens = list(gens)
                    while gens:
                        for g in list(gens):
                            try:
                                next(g)
                            except StopIteration:
                                gens.remove(g)

                if True:
                    rr([seq(gen_pair(0, TS[0], 0), gen_pair(2, TS[0], 0)),
                        seq(gen_pair(1, TS[1], 0), gen_pair(3, TS[1], 0)), gen_lora()])
                if s == NST - 1 and pre_a2:
                    assert pre_a2["end"] <= w1_end, (pre_a2["end"], w1_end)
                    for kc in range(8):
                        S.dma(pre_a2["w2"][:, kc, :], w_in[kc * 128:(kc + 1) * 128, 1792:3328], eng="pool")
                    for kc in range(4):
                        S.dma(pre_a2["wob"][:, kc, :], w_out[512 + kc * 128:512 + (kc + 1) * 128, :], eng="pool")
                    pre_a2["done"] = True
                def gen_tok():
                    for j in range(4):
                        for (src_, dst) in ((vT, tokV), (bT, tokB), (kT, tokK)):
                            tq = tqhalf()
                            for c in range(4):
                                S.tr(tq[:, c * 128:(c + 1) * 128], src_[:, c, j * 128:(j + 1) * 128], ident)
                            S.copy("act" if dst is tokB else "dve", dst[:, j, :], tq)
                            yield
                def gen_chain(j, gi):
                    tj = slice(j * 128, (j + 1) * 128)
                    ch = CH[(j % 2) * 2 + gi]
                    Mb_, MTb_, Pb_ = ch["M"], ch["MT"], ch["P"]
                    AkT_, RbT_, RkT_ = AM[j % 2]
                    r0 = gi * 64
                    rows = slice(r0, r0 + 64)
                    gsl = slice(gi * 512, (gi + 1) * 512)
                    for typ in range(5):
                        b = bank()
                        for hh in range(4):
                            A_, B_, K_, R_ = aT[rows, hh, tj], bT[rows, hh, tj], kT[rows, hh, tj], rT[rows, hh, tj]
                            lhsT, rhs = ((B_, A_), (A_, B_), (K_, A_), (B_, R_), (K_, R_))[typ]
                            S.mm(b[:, hh * 128:(hh + 1) * 128], lhsT, rhs)
                        if typ == 0:
                            S.tt("dve", Mb_[0], b[:], SU4, ALU.mult)
                            S.tt("pool", Pb_[0], Mb_[0], I4, ALU.add)
                        elif typ == 1:
                            S.tt("dve", MTb_[0], b[:], SL4, ALU.mult)
                        elif typ == 2:
                            S.tt("dve", AkT_[:, gsl], b[:], SU4, ALU.mult)
                        elif typ == 3:
                            S.tt("dve", RbT_[:, gsl], b[:], UI4, ALU.mult)
                        else:
                            S.tt("dve", RkT_[:, gsl], b[:], UI4, ALU.mult)
                        yield
                    for l in range(1, 7):
                        sr, ds = (l - 1) % 2, l % 2
                        if l < 6:
                            b = bank()
                            for hh in range(4):
                                q = slice(hh * 128, (hh + 1) * 128)
                                S.mm(b[:, q], MTb_[sr][:, q], Mb_[sr][:, q])
                            S.copy("act", Mb_[ds], b[:])
                        b = bank()
                        for hh in range(4):
                            q = slice(hh * 128, (hh + 1) * 128)
                            S.mm(b[:, q], Mb_[sr][:, q], MTb_[sr][:, q])
                        S.copy("act", MTb_[ds], b[:])
                        yield
                        b = bank()
                        for hh in range(4):
                            q = slice(hh * 128, (hh + 1) * 128)
                            S.mm(b[:, q], MTb_[ds][:, q], Pb_[sr][:, q])
                        S.tt("dve", Pb_[ds], b[:], Pb_[sr], ALU.add)
                        yield

                def gen_rec(j):
                    tj = slice(j * 128, (j + 1) * 128)
                    AkT_, RbT_, RkT_ = AM[j % 2]
                    TT = [CH[(j % 2) * 2 + gi]["P"][0] for gi in range(2)]
                    for gi in range(2):
                        r0 = gi * 64
                        rows = slice(r0, r0 + 64)
                        bG = bank()
                        for hh in range(4):
                            fs = slice(hh * 128 + r0, hh * 128 + r0 + 64)
                            o = bG[:, hh * 64:(hh + 1) * 64]
                            S.mm(o, aT[rows, hh, tj], S0bf[rows, hh * 64:(hh + 1) * 64], start=True, stop=False)
                            S.mm(o, AkT_[:, gi * 512 + hh * 128:gi * 512 + (hh + 1) * 128], tokV[:, j, fs],
                                 start=False, stop=True)
                        S.copy("act" if gi == 0 else "dve", Gbf[:, gi * 256:(gi + 1) * 256], bG[:, 0:256])
                    yield
                    bU = bank()
                    for gi in range(2):
                        for hh in range(4):
                            sl_ = slice(gi * 256 + hh * 64, gi * 256 + (hh + 1) * 64)
                            S.mm(bU[:, sl_], TT[gi][:, hh * 128:(hh + 1) * 128], Gbf[:, sl_])
                    S.copy("act", Ubf, bU[:])
                    yield
                    bS = bank()
                    for gi in range(2):
                        r0 = gi * 64
                        rows = slice(r0, r0 + 64)
                        for hh in range(4):
                            fs = slice(hh * 128 + r0, hh * 128 + r0 + 64)
                            sl_ = slice(gi * 256 + hh * 64, gi * 256 + (hh + 1) * 64)
                            o = bS[rows, hh * 64:(hh + 1) * 64]
                            S.mm(o, tokB[:, j, fs], Ubf[:, sl_], start=True, stop=False)
                            S.mm(o, tokK[:, j, fs], tokV[:, j, fs], start=False, stop=True)
                    for gi in range(2):
                        r0 = gi * 64
                        rows = slice(r0, r0 + 64)
                        bY = bank()
                        for hh in range(4):
                            fs = slice(hh * 128 + r0, hh * 128 + r0 + 64)
                            sl_ = slice(gi * 256 + hh * 64, gi * 256 + (hh + 1) * 64)
                            o = bY[rows, hh * 128:(hh + 1) * 128]
                            S.mm(o, S0bf[rows, hh * 64:(hh + 1) * 64], rT[rows, hh, tj], start=True, stop=False)
                            S.mm(o, Ubf[:, sl_], RbT_[:, gi * 512 + hh * 128:gi * 512 + (hh + 1) * 128],
                                 start=False, stop=False)
                            S.mm(o, tokV[:, j, fs], RkT_[:, gi * 512 + hh * 128:gi * 512 + (hh + 1) * 128],
                                 start=False, stop=True)
                        S.copy("act", ysb[rows, :, tj], bY[rows, :].rearrange("p (c t) -> p c t", c=4))
                    S.tt("dve", tmpS, bS[:, 0:256], S0T, ALU.add)
                    S.tt("dve", S0T.rearrange("p (c v) -> p c v", c=4),
                         tmpS.rearrange("p (c v) -> p c v", c=4),
                         Wc[:, :, j:j + 1].to_broadcast([128, 4, 64]), ALU.mult)
                    S.copy("pool", S0bf, S0T)
                    yield

                def seq(*gs):
                    for g in gs:
                        yield from g

                def rr(gens, bg=None):
                    gens = list(gens)
                    while gens:
                        for g in list(gens):
                            try:
                                next(g)
                            except StopIteration:
                                gens.remove(g)
                        if bg is not None:
                            next(bg, None)

                rr([gen_chain(0, 0), gen_chain(0, 1), gen_chain(1, 0), gen_chain(1, 1), gen_tok()], bg=bgn)
                rr([gen_rec(0)], bg=bgn)
                rr([gen_rec(1), gen_chain(2, 0), gen_chain(2, 1)], bg=bgn)
                rr([gen_rec(2), gen_chain(3, 0), gen_chain(3, 1)], bg=bgn)
                rr([gen_rec(3)], bg=bgn)
                for _ in bgn:
                    pass
                def gen_gn(c, d, dsq, delay):
                    for _ in range(delay):
                        yield
                    b1 = bank()
                    S.mm(b1[:], blkm, ysb[:, c, :])
                    yield
                    S.tt("dve", d, ysb[:, c, :], b1[:], ALU.subtract)
                    yield
                    S.tt("pool", dsq, d, d, ALU.mult)
                    yield
                    b2 = bank()
                    S.mm(b2[:], blkm, dsq)
                    yield
                    S.act(dsq, b2[:], AF.Ln, bias=epsln[:, 0:1])
                    yield
                    S.act(dsq, dsq, AF.Exp, scale=-0.5)
                    yield
                    S.tt("dve", d, d, dsq, ALU.mult)
                    S.ts("dve", d, d, pv[:, PV_LNW + c:PV_LNW + c + 1], pv[:, PV_LNB + c:PV_LNB + c + 1],
                         ALU.mult, ALU.add)
                    yield
                    S.tt("pool", d, d, bon[:, c, :], ALU.add)
                    yield
                    S.tt("dve", mixA[:, c, :], d, gsb[:, c, :], ALU.mult)
                    yield

                if True:
                    rr([seq(gen_gn(0, tl[0], tl[1], 0), gen_gn(2, tl[0], tl[1], 0)),
                        seq(gen_gn(1, tlb[4], tlb[5], 2), gen_gn(3, tlb[4], tlb[5], 0))])
                xt_ = {}
                for j in range(2):
                    xt_[j] = xinA[j % 2]
                    S.dma(xt_[j], x[t0 + j * 128:t0 + (j + 1) * 128, :])
                for j in range(4):
                    tj = slice(j * 128, (j + 1) * 128)
                    xt = xt_[j]
                    for half in range(2):
                        b = bank()
                        hsl = slice(half * 512, (half + 1) * 512)
                        for c in range(4):
                            S.mm(b[:], mixA[:, c, tj], woa[:, c, hsl], start=(c == 0), stop=(c == 3))
                        S.tt("dve", xt[:, hsl], b[:], xt[:, hsl], ALU.add)
                    S.dma(out[t0 + j * 128:t0 + (j + 1) * 128, :], xt, is_output=True)
                    if j + 2 < 4:
                        xt_[j + 2] = xinA[j % 2]
                        S.dma(xt_[j + 2], x[t0 + (j + 2) * 128:t0 + (j + 3) * 128, :])
        src_a2 = out if do_a1 else x

        if do_a2:
            al.p = base_mark
            w2 = al.bf(8 * 1536).rearrange("p (k n) -> p k n", k=8)
            wob = al.bf(4 * 1024).rearrange("p (k n) -> p k n", k=4)
            KA = [al.bf(4 * T).rearrange("p (h t) -> p h t", h=4) for _ in range(2)]
            Vat = al.bf(NKT * 4 * 130).rearrange("p (k h d) -> p k h d", k=NKT, h=4)
            Dt = al.f32(128)
            rs2 = [al.f32(512) for _ in range(2)]
            hn3 = al.bf(1024)
            subr = al.f32(128)
            lmv = al.f32(256)
            xin = [al.f32(1024) for _ in range(2)]
            QA = [[al.bf(2048).rearrange("p (h t) -> p h t", h=4) for _ in range(2)] for _ in range(2)]
            mixB = al.bf(2048).rearrange("p (h t) -> p h t", h=4)
            pT = [al.bf(512) for _ in range(4)]
            tn = [al.f32(512).rearrange("p (q d) -> p q d", q=4) for _ in range(2)]
            etmp = [al.f32(512) for _ in range(2)]
            tl2 = [al.f32(512) for _ in range(2)]
            Ot = al.f32(512).rearrange("p (q d) -> p q d", q=4)
            Obf = al.bf(512).rearrange("p (q d) -> p q d", q=4)
            sm = al.f32(32)
            hn2 = al.bf(1024)
            NT["hn"] = hn2
            NT["junk"] = tl2[1].bitcast(BF16)
            eps6 = al.f32(2)
            S.memset("pool", eps6, 1e-6)
            print("A2 arena words", al.p)

            if not pre_a2.get("done"):
                for kc in range(8):
                    S.dma(w2[:, kc, :], w_in[kc * 128:(kc + 1) * 128, 1792:3328], eng="pool")
                for kc in range(4):
                    S.dma(wob[:, kc, :], w_out[512 + kc * 128:512 + (kc + 1) * 128, :], eng="pool")
            S.dma(Dt, cst[:, C_D:C_D + 128])
            for m in range(2):
                ar0 = 64 * (1 - m)
                S.memset("dve", KA[m].rearrange("p h t -> p (h t)"), 0.0)
                for qq in range(2):
                    S.memset("dve", QA[qq][m].rearrange("p h t -> p (h t)"), 0.0)
                for kt in range(NKT):
                    S.copy("pool", KA[m][ar0:ar0 + 3, :, kt * 128:(kt + 1) * 128], kaug[ar0:ar0 + 3, :, :])
                for qq in range(2):
                    for h in range(4):
                        S.copy("pool", QA[qq][m][ar0:ar0 + 3, h, :], qaug[ar0:ar0 + 3, :])
            S.dma(subr, subw)
            S.dma(lmv, lamv)
            S.ts("dve", subr, subr, 1.0 - LAMBDA_INIT, None, ALU.mult)
            S.tt("dve", tl2[0][:, 0:64], lmv[:, 0:64], lmv[:, 64:128], ALU.mult)
            S.tt("dve", tl2[0][:, 64:128], lmv[:, 128:192], lmv[:, 192:256], ALU.mult)
            S.reduce(sm[:, 0:2], tl2[0][:, 0:128].rearrange("p (a d) -> p a d", a=2), ALU.add)
            S.act(sm[:, 2:4], sm[:, 0:2], AF.Exp)
            S.tt("dve", sm[:, 4:5], sm[:, 2:3], sm[:, 3:4], ALU.subtract)
            S.ts("dve", nlam, sm[:, 4:5], -1.0, -LAMBDA_INIT, ALU.mult, ALU.add)
            for kt in range(NKT):
                S.memset("pool", Vat[:, kt, :, 128:129], 1.0)

            gb_state = {"i": 0}
            gbanks = [pbs[5], tqs[0][:].bitcast(F32)]

            def gbank():
                gb_state["i"] += 1
                return gbanks[gb_state["i"] % 2]

            def gen_pre(s):
                t0 = s * 512
                qa = QA[s % 2]

                def xload(j):
                    xt = xin[j % 2]
                    S.dma(xt, x[t0 + j * 128:t0 + (j + 1) * 128, :])
                    return xt
                yield from norm_T_gen(xload, PV_ANW, tqb=[tqs[1]], hns=[hn2, hn3], defer=3)

                def gen_qk(chunks, raw, sq, delay):
                    for _ in range(delay):
                        yield
                    for (h, which) in chunks:
                        b = gbank()
                        col0 = which * 512 + h * 128
                        for kc in range(8):
                            S.mm(b[:, :], w2[:, kc, col0:col0 + 128], hT[:, kc, :], start=(kc == 0), stop=(kc == 7))
                        yield
                        S.copy("dve", raw, b[:, :])
                        S.act(sq, b[:, :], AF.Square)
                        yield
                        b2 = gbank()
                        S.mm(b2[:, :], blkm, sq)
                        yield
                        S.act(sq, b2[:, :], AF.Ln, bias=eps6[:, 0:1])
                        yield
                        S.act(sq, sq, AF.Exp, scale=-0.5)
                        yield
                        for m in range(2):
                            rr_ = slice(m * 64, (m + 1) * 64)
                            if which == 0:
                                S.stt(qa[m][rr_, h, :], raw[rr_, :], qws[rr_, :], sq[rr_, :], ALU.mult, ALU.mult)
                            else:
                                S.stt(KA[m][rr_, h, t0:t0 + 512], raw[rr_, :], kws[rr_, :], sq[rr_, :],
                                      ALU.mult, ALU.mult)
                        yield
                chunks = [(h, which) for h in range(4) for which in range(2)]
                ga = gen_qk(chunks[0::2], tl2[0], tl2[1], 0)
                gb = gen_qk(chunks[1::2], rs2[0], rs2[1], 3)
                live = [ga, gb]
                while live:
                    for g in list(live):
                        try:
                            next(g)
                        except StopIteration:
                            live.remove(g)
                    yield
                for j in range(4):
                    b = gbank()
                    for kc in range(8):
                        S.mm(b[:, :], hT[:, kc, j * 128:(j + 1) * 128], w2[:, kc, 1024:1536],
                             start=(kc == 0), stop=(kc == 7))
                    yield
                    S.copy("dve", Vat[:, 4 * s + j, :, 0:128], b[:, :].rearrange("p (h d) -> p h d", h=4))
                    yield

            def gen_post(s):
                t0 = s * 512
                for j in range(4):
                    tj = slice(j * 128, (j + 1) * 128)
                    xt = xin[j % 2]
                    S.dma(xt, src_a2[t0 + j * 128:t0 + (j + 1) * 128, :])
                    bs_ = []
                    for half in range(2):
                        b = gbank()
                        bs_.append(b)
                        hsl = slice(half * 512, (half + 1) * 512)
                        for c in range(4):
                            S.mm(b[:, :], mixB[:, c, tj], wob[:, c, hsl], start=(c == 0), stop=(c == 3))
                    for half in range(2):
                        hsl = slice(half * 512, (half + 1) * 512)
                        S.tt("dve", xt[:, hsl], bs_[half][:, :], xt[:, hsl], ALU.add)
                    S.dma(out[t0 + j * 128:t0 + (j + 1) * 128, :], xt, is_output=True)
                    yield

            def gen_loop(s):
                qa = QA[s % 2]
                nkt = 4 * s + 4
                steps = [(h, m, kt) for h in range(4) for m in range(2) for kt in range(nkt)]
                scb = [pbs[2], pbs[3], pbs[4]]
                accp = [pbs[0], pbs[1]]
                LA = 2
                started = {}

                def emit_score(i):
                    h, m, kt = steps[i]
                    jd = kt - 4 * s
                    q0 = 128 * jd if jd > 0 else 0
                    sc = scb[i % 3]
                    r0 = m * 64
                    rows = slice(r0, r0 + 64)
                    S.mm(sc[:, q0:512], KA[m][:, h, kt * 128:(kt + 1) * 128], qa[m][:, h, q0:512])

                def emit_rest(i):
                    h, m, kt = steps[i]
                    u = h * 2 + m
                    jd = kt - 4 * s
                    q0 = 128 * jd if jd > 0 else 0
                    sc = scb[i % 3]
                    pt = pT[i % 4]
                    cimm = -SLOPES[h] * (512 * s - 128 * kt - 128)
                    if jd >= 0:
                        blk = sc[:, q0:q0 + 128]
                        S.stt(blk, Dt[:, 0:128], float(SLOPES[h]), blk, ALU.mult, ALU.add)
                    S.act(pt[:, q0:512], sc[:, q0:512], AF.Exp, bias=float(cimm))
                    for qj in range(max(jd, 0), 4):
                        ab = accp[qj // 2]
                        o = ab[:, (qj % 2) * 130:(qj % 2) * 130 + 129]
                        key = (u, qj // 2)
                        st = key not in started
                        started[key] = True
                        S.mm(o, pt[:, qj * 128:(qj + 1) * 128], Vat[:, kt, h, 0:129],
                             start=st, stop=(kt == 4 * s + qj), skip=True)
                    if kt == nkt - 1:
                        for bb in range(2):
                            S.recip(sm[:, 8 + bb * 2:8 + bb * 2 + 2], accp[bb][:, 128:260:130])
                            S.tt("dve", tn[m][:, 2 * bb:2 * bb + 2, :],
                                 accp[bb][:, 0:260].rearrange("p (q d) -> p q d", q=2)[:, :, 0:128],
                                 sm[:, 8 + bb * 2:8 + bb * 2 + 2].unsqueeze(2).to_broadcast([128, 2, 128]), ALU.mult)
                        if m == 1:
                            Of = Ot.rearrange("p q d -> p (q d)")
                            S.stt(Of, tn[1].rearrange("p q d -> p (q d)"), nlam,
                                  tn[0].rearrange("p q d -> p (q d)"), ALU.mult, ALU.add)
                            S.tt("pool", tn[0].rearrange("p q d -> p (q d)"), Of, Of, ALU.mult)
                            S.reduce(sm[:, 16:20], tn[0], ALU.add)
                            S.ts("dve", sm[:, 20:24], sm[:, 16:20], 1.0 / 128.0, 1e-6, ALU.mult, ALU.add)

                            def stage_b():
                                S.rsqrt(sm[:, 24:28], sm[:, 20:24])

                            def stage_c(h=h):
                                S.tt("dve", Ot, Ot, sm[:, 24:28].unsqueeze(2).to_broadcast([128, 4, 128]), ALU.mult)
                                S.tt("pool", Obf, Ot, subr.unsqueeze(1).to_broadcast([128, 4, 128]), ALU.mult)
                                tq = tqs[1][:, 0:512]
                                for qj in range(4):
                                    S.tr(tq[:, qj * 128:(qj + 1) * 128], Obf[:, qj, :], ident)
                                S.copy("dve", mixB[:, h, :], tq)
                            deferred.append((i + LA + dB, stage_b))
                            deferred.append((i + LA + dC, stage_c))

                dB, dC = (3, 5) if nkt < 8 else (8, 12)
                deferred = []

                def run_deferred(now):
                    while deferred and deferred[0][0] <= now:
                        deferred.pop(0)[1]()

                for i in range(len(steps) + LA):
                    if i < len(steps):
                        emit_score(i)
                    if i >= LA:
                        emit_rest(i - LA)
                    run_deferred(i)
                    yield
                run_deferred(1 << 30)

            def seq2(*gs):
                for g in gs:
                    yield from g

            def rr2(gens):
                gens = [g if isinstance(g, tuple) else (g, 1) for g in gens]
                while gens:
                    for g in list(gens):
                        try:
                            for _ in range(g[1]):
                                next(g[0])
                        except StopIteration:
                            gens.remove(g)

            rr2([gen_pre(0)])
            for s in range(NST):
                others = []
                if s > 0:
                    others.append(gen_post(s - 1))
                if s + 1 < NST:
                    others.append(gen_pre(s + 1))
                nsteps = 8 * (4 * s + 4)
                rr2([(gen_loop(s), 3 if s < 6 else 4), (seq2(*others), 1)])
            rr2([gen_post(NST - 1)])
        src_b = out if (do_a1 or do_a2) else x

        if do_b:
            al.p = base_mark
            NT["hn"] = al.bf(1024)
            NT["junk"] = al.bf(1024)
            wup = al.bf(8 * 5632).rearrange("p (k n) -> p k n", k=8)
            wdn = al.bf(22 * 1024).rearrange("p (k n) -> p k n", k=22)
            gT = al.bf(22 * 512).rearrange("p (c t) -> p c t", c=22)
            xin = [al.f32(1024) for _ in range(2)]
            ccab = [al.f32(88).rearrange("p (c k) -> p c k", c=44) for _ in range(2)]
            tb = [al.f32(512) for _ in range(3)]
            hT2 = al.bf(4096).rearrange("p (k t) -> p k t", k=8)
            hnB2 = al.bf(1024)
            print("B arena words", al.p)
            for kc in range(8):
                for part in range(2):
                    S.dma(wup[:, kc, part * 2816:(part + 1) * 2816],
                          w_up[kc * 128:(kc + 1) * 128, part * 2816:(part + 1) * 2816], eng="pool")
            for kc in range(22):
                S.dma(wdn[:, kc, :], w_dn[kc * 128:(kc + 1) * 128, :], eng="pool")
            for cc_ in ccab:
                S.memset("pool", cc_.rearrange("p c k -> p (c k)"), 0.0)

            hTb = [hT, hT2]

            def gen_normB(s):
                t0 = s * 512

                def xload(j):
                    xt = xin[j % 2]
                    S.dma(xt, src_b[t0 + j * 128:t0 + (j + 1) * 128, :])
                    return xt
                yield from norm_T_gen(xload, PV_FNW, hT_=hTb[s % 2], defer=9, hns=[NT["hn"], hnB2],
                                      pool_rsqrt=True)

            def gen_ffn(s):
                hcur = hTb[s % 2]
                n = 0
                for cg in range(22):
                    res = []
                    for ch in (cg, 22 + cg):
                        b = bank()
                        for kc in range(8):
                            S.mm(b[:], wup[:, kc, ch * 128:(ch + 1) * 128], hcur[:, kc, :],
                                 start=(kc == 0), stop=(kc == 7))
                        t1 = tb[n % 2]
                        n += 1
                        cold, cnew = ccab[(s + 1) % 2], ccab[s % 2]
                        w0_ = pv[:, PV_CW0 + ch:PV_CW0 + ch + 1]
                        w1_ = pv[:, PV_CW1 + ch:PV_CW1 + ch + 1]
                        S.act(t1, b[:], AF.Identity, scale=pv[:, PV_CW2 + ch:PV_CW2 + ch + 1],
                              bias=pv[:, PV_CB + ch:PV_CB + ch + 1])
                        S.copy("act", cnew[:, ch, :], b[:, 510:512])
                        S.stt(t1[:, 1:512], b[:, 0:511], w1_, t1[:, 1:512], ALU.mult, ALU.add)
                        S.stt(t1[:, 0:1], cold[:, ch, 1:2], w1_, t1[:, 0:1], ALU.mult, ALU.add)
                        S.stt(t1[:, 2:512], b[:, 0:510], w0_, t1[:, 2:512], ALU.mult, ALU.add)
                        S.stt(t1[:, 0:2], cold[:, ch, 0:2], w0_, t1[:, 0:2], ALU.mult, ALU.add)
                        res.append(t1)
                        yield
                    sg = tb[2]
                    S.act(sg, res[0], AF.Silu)
                    S.tt("pool", gT[:, cg, :], sg, res[1], ALU.mult)

            def postB(s):
                t0 = s * 512
                tiles = {}
                for j in range(2):
                    tiles[j] = xin[j % 2]
                    S.dma(tiles[j], src_b[t0 + j * 128:t0 + (j + 1) * 128, :])
                for j in range(4):
                    tj = slice(j * 128, (j + 1) * 128)
                    xt = tiles[j]
                    for half in range(2):
                        b = bank()
                        hsl = slice(half * 512, (half + 1) * 512)
                        for cg in range(22):
                            S.mm(b[:], gT[:, cg, tj], wdn[:, cg, hsl], start=(cg == 0), stop=(cg == 21))
                        S.tt("dve", xt[:, hsl], b[:], xt[:, hsl], ALU.add)
                    S.dma(out[t0 + j * 128:t0 + (j + 1) * 128, :], xt, is_output=True)
                    if j + 2 < 4:
                        tiles[j + 2] = xin[j % 2]
                        S.dma(tiles[j + 2], src_b[t0 + (j + 2) * 128:t0 + (j + 3) * 128, :])

            def rrB(gens):
                gens = list(gens)
                while gens:
                    for g in list(gens):
                        try:
                            next(g)
                        except StopIteration:
                            gens.remove(g)

            rrB([gen_normB(0)])
            for s in range(NST):
                gs = [gen_ffn(s)]
                if s + 1 < NST:
                    gs.append(gen_normB(s + 1))
                rrB(gs)
                postB(s)

        print("arena hi words", al.hi, "of", ARENA_F32, "n ops", {e: len(S.ops[e]) for e in ENGS})
        sems = {e: es.enter_context(nc.semaphore("s_" + e)) for e in ENGS}
        dsems = [es.enter_context(nc.semaphore("d%d" % i)) for i in range(S.n_slots)]
        with nc.Block() as block:
            S.emit(nc, block, sems, dsems)
    return nc


def _cols(v, nchunk):
    return np.ascontiguousarray(np.asarray(v, np.float32).reshape(nchunk, 128).T)


def host_consts():
    p = np.arange(128)
    blk1 = (p[:, None] // 64 == p[None, :] // 64).astype(np.float32)
    cst = np.zeros((128, NCST), np.float32)
    cst[:, C_BLK1:C_BLK1 + 128] = blk1
    cst[:, C_BLKM:C_BLKM + 128] = blk1 / 64.0
    cst[:, C_NH:C_NH + 512] = -0.5
    xq = np.arange(640)
    allowed = (p[:, None] // 64) <= (xq[None, :] // 64)
    d = 2.0 * np.minimum(xq[None, :] - p[:, None], 0).astype(np.float32)
    cst[:, C_D:C_D + 640] = np.where(allowed, d, -1.0e6)
    cb = np.zeros((128, NCB), np.float32)
    col = np.arange(128)
    eye = np.eye(128, dtype=np.float32)
    su = (col[None, :] > p[:, None]).astype(np.float32)
    sl = (col[None, :] < p[:, None]).astype(np.float32)
    ui = (col[None, :] >= p[:, None]).astype(np.float32)
    cb[:, CB_ID:CB_ID + 128] = eye
    cb[:, CB_SU:CB_SU + 512] = np.tile(su, (1, 4))
    cb[:, CB_SL:CB_SL + 512] = np.tile(sl, (1, 4))
    cb[:, CB_UI:CB_UI + 512] = np.tile(ui, (1, 4))
    cb[:, CB_I4:CB_I4 + 512] = np.tile(eye, (1, 4))
    rm = np.ones(512, np.float32)
    rm[0::128] = 0.0
    cb[:, CB_RM:CB_RM + 512] = rm[None, :]
    ql = np.arange(512)
    for r0 in (0, 64):
        for h in range(4):
            cb[r0 + 0, CB_KAUG + h * 128:CB_KAUG + (h + 1) * 128] = SLOPES[h]
            cb[r0 + 1, CB_KAUG + h * 128:CB_KAUG + (h + 1) * 128] = SLOPES[h]
            cb[r0 + 2, CB_KAUG + h * 128:CB_KAUG + (h + 1) * 128] = SLOPES[h] * np.arange(128)
        cb[r0 + 0, CB_QAUG:CB_QAUG + 512] = -(128.0 + 128.0 * (ql // 128))
        cb[r0 + 1, CB_QAUG:CB_QAUG + 512] = -(ql % 128).astype(np.float32)
        cb[r0 + 2, CB_QAUG:CB_QAUG + 512] = 1.0
    return cst, cb


def host_params(inp):
    g = lambda k: np.asarray(inp[k], np.float32)[0]
    pvec = np.zeros((128, NPV), np.float32)
    pvec[:, PV_ANW:PV_ANW + 8] = _cols(g("attn_norm_w"), 8)
    pvec[:, PV_FNW:PV_FNW + 8] = _cols(g("ffn_norm_w"), 8)
    pvec[:, PV_MU:PV_MU + 14] = _cols(g("mu_shift"), 14)
    pvec[:, PV_W0:PV_W0 + 4] = _cols(g("w0"), 4)
    pvec[:, PV_A0:PV_A0 + 4] = _cols(g("a0"), 4)
    pvec[:, PV_KK:PV_KK + 4] = _cols(g("k_k"), 4)
    pvec[:, PV_KA:PV_KA + 4] = _cols(g("k_a"), 4)
    pvec[:, PV_RK:PV_RK + 4] = _cols(g("r_k").reshape(-1), 4)
    pvec[:, PV_LNW:PV_LNW + 4] = _cols(g("ln_x_w"), 4)
    pvec[:, PV_LNB:PV_LNB + 4] = _cols(g("ln_x_b"), 4)
    pvec[:, PV_QW] = np.tile(g("q_norm_w"), 2)
    pvec[:, PV_KW] = np.tile(g("k_norm_w"), 2)
    cw = g("ffn_conv_w")
    pvec[:, PV_CW0:PV_CW0 + 44] = _cols(cw[0], 44)
    pvec[:, PV_CW1:PV_CW1 + 44] = _cols(cw[1], 44)
    pvec[:, PV_CW2:PV_CW2 + 44] = _cols(cw[2], 44)
    pvec[:, PV_CB:PV_CB + 44] = _cols(g("ffn_conv_b"), 44)
    lamv = np.concatenate([g("lambda_q1"), g("lambda_k1"), g("lambda_q2"), g("lambda_k2")])
    lamv = np.ascontiguousarray(np.broadcast_to(lamv[None, :], (128, 256)))
    subw = np.ascontiguousarray(np.broadcast_to(g("subln_w")[None, :], (128, 128)))
    w_lora = np.ascontiguousarray(np.concatenate([g("w_decay_up"), g("w_aaa_up")], axis=0))
    cst, cb = host_consts()
    return {
        "w_in": np.ascontiguousarray(g("w_in")), "w_out": np.ascontiguousarray(g("w_out")),
        "w_up": np.ascontiguousarray(g("w_ffn_up")), "w_dn": np.ascontiguousarray(g("w_ffn_down")),
        "w_lora": w_lora, "w_gate": np.ascontiguousarray(g("w_gate_up")),
        "pvec": pvec, "cst": cst, "cstb": cb, "lamv": lamv, "subw": subw,
    }


_NC_CACHE = {}


def run(inputs, T, n_cores, **flags):
    key = (T, tuple(sorted(flags.items())))
    if key not in _NC_CACHE:
        _NC_CACHE[key] = build(T, **flags)
    nc = _NC_CACHE[key]
    shared = host_params(inputs)
    xfull = np.asarray(inputs["x"], np.float32)
    in_maps = []
    for b in range(n_cores):
        m = dict(shared)
        m["x"] = np.ascontiguousarray(xfull[b, :T])
        in_maps.append(m)
    res = run_bass_kernel_spmd(nc, in_maps, core_ids=list(range(n_cores)))
    return np.stack([np.asarray(r["out"]) for r in res.results], axis=0)


def kernel(**inputs):
    return run(inputs, 4096, 8).astype(np.float32)
```

```python
import numpy as np
import concourse.bass as bass
import concourse.mybir as mybir
from concourse.bass_utils import run_bass_kernel_spmd

F32 = mybir.dt.float32
BF16 = mybir.dt.bfloat16
ALU = mybir.AluOpType
AF = mybir.ActivationFunctionType
AX = mybir.AxisListType

ENGS = ("pe", "act", "dve", "pool", "sp")


class Op:
    __slots__ = ("eng", "seq", "fn", "waits", "signal", "is_dma", "slot", "slotval", "semval", "gid")

    def __init__(self, eng, seq, fn, is_dma):
        self.eng = eng
        self.seq = seq
        self.fn = fn
        self.waits = []
        self.signal = False
        self.is_dma = is_dma
        self.slot = None
        self.slotval = None
        self.semval = None


def _footprint(ap):
    t = ap.tensor
    name = t.name
    space = str(ap.space)
    dims = ap.ap
    off = int(ap.offset)
    isz = mybir.dt.size(ap.dtype)
    if space == "PSUM":
        return name, True, (0, 128, 0, 1 << 30)
    if space == "SB":
        psz = dims[0][0]
        if psz <= 0:
            psz = 1
            for s in t.shape[1:]:
                psz *= int(s)
        p0 = off // psz
        f0 = off % psz
        pc = dims[0][1]
        ext = 1
        for st, cnt in dims[1:]:
            ext += (cnt - 1) * abs(st)
        return name, False, (p0, p0 + pc, f0 * isz, (f0 + ext) * isz)
    ext = 1
    for st, cnt in dims:
        ext += (cnt - 1) * abs(st)
    return name, False, (0, 1, off * isz, (off + ext) * isz)


class Sched:
    def __init__(self, n_slots=40):
        self.ops = {e: [] for e in ENGS}
        self.known = {e: {} for e in ENGS}
        self.recs = {}
        self.n_slots = n_slots
        self.n_dma = 0
        self.n_dma_q = {}
        self.slot_last = [None] * n_slots
        self.all_dma_out = []

    def _need(self, op, dep):
        if dep is None or dep is op:
            return
        if dep.is_dma:
            key = ("dma", dep.slot)
            val = dep.slotval
        else:
            if dep.eng == op.eng and op.eng == "pe":
                return
            key = dep.eng
            val = dep.seq
        kn = self.known[op.eng]
        if kn.get(key, -1) >= val:
            return
        kn[key] = val
        dep.signal = True
        for i, (k, d) in enumerate(op.waits):
            if k == key:
                op.waits[i] = (key, dep)
                return
        op.waits.append((key, dep))

    def _access(self, op, ap, is_write):
        name, is_psum, fp = _footprint(ap)
        recs = self.recs.setdefault(name, [])
        p0, p1, f0, f1 = fp
        keep = []
        for r in recs:
            rp0, rp1, rf0, rf1, wr, rds = r
            if rp0 < p1 and p0 < rp1 and rf0 < f1 and f0 < rf1:
                if wr is not None:
                    self._need(op, wr)
                if is_write or is_psum:
                    for rd in rds:
                        self._need(op, rd)
                if is_write and rp0 >= p0 and rp1 <= p1 and rf0 >= f0 and rf1 <= f1:
                    continue
            keep.append(r)
        recs[:] = keep
        if is_write:
            recs.append([p0, p1, f0, f1, op, []])
        else:
            for r in recs:
                if r[0] == p0 and r[1] == p1 and r[2] == f0 and r[3] == f1:
                    rds = r[5]
                    if not op.is_dma:
                        rds[:] = [x for x in rds if x.eng != op.eng or x.is_dma]
                    rds.append(op)
                    return
            recs.append([p0, p1, f0, f1, None, [op]])

    def add(self, eng, fn, reads=(), writes=(), is_dma=False):
        op = Op(eng, len(self.ops[eng]), fn, is_dma)
        if is_dma:
            n = self.n_dma_q.get(eng, 0)
            self.n_dma_q[eng] = n + 1
            self.n_dma += 1
            if eng == "pool":
                base, cnt = self.n_slots - 12, 12
            else:
                base, cnt = 0, self.n_slots - 12
            op.slot = base + n % cnt
            op.slotval = 16 * (n // cnt + 1)
            prev = self.slot_last[op.slot]
            if prev is not None:
                self._need(op, prev)
            self.slot_last[op.slot] = op
        for ap in reads:
            self._access(op, ap, False)
        for ap in writes:
            self._access(op, ap, True)
        self.ops[eng].append(op)
        return op

    def dma(self, out, in_, eng="sp", is_output=False):
        op = self.add(eng, lambda e: e.dma_start(out=out, in_=in_), [in_], [out], is_dma=True)
        if is_output:
            self.all_dma_out.append(op)
        return op

    def mm(self, out, lhsT, rhs, start=True, stop=True, skip=False):
        return self.add("pe", lambda e: e.matmul(out, lhsT, rhs, start=start, stop=stop,
                                                 skip_group_check=skip), [lhsT, rhs], [out])

    def tr(self, out, in_, ident):
        return self.add("pe", lambda e: e.transpose(out, in_, ident), [in_, ident], [out])

    def act(self, out, in_, func, bias=0.0, scale=1.0, accum=None, eng="act"):
        rd = [in_]
        if not isinstance(bias, (int, float)):
            rd.append(bias)
        if not isinstance(scale, (int, float)):
            rd.append(scale)
        wr = [out]
        if accum is not None:
            wr.append(accum)

        def fn(e):
            if accum is not None:
                return e.activation(out, in_, func, bias=bias, scale=scale, accum_out=accum)
            return e.activation(out, in_, func, bias=bias, scale=scale)
        return self.add("act", fn, rd, wr)

    def tt(self, eng, out, in0, in1, op):
        return self.add(eng, lambda e: e.tensor_tensor(out, in0, in1, op), [in0, in1], [out])

    def ts(self, eng, out, in0, s1, s2, op0, op1=None, accum=None):
        rd = [in0]
        if not isinstance(s1, (int, float)):
            rd.append(s1)
        if s2 is not None and not isinstance(s2, (int, float)):
            rd.append(s2)
        wr = [out]
        if accum is not None:
            wr.append(accum)

        def fn(e):
            kw = {}
            if accum is not None:
                kw["accum_out"] = accum
            if op1 is None:
                return e.tensor_scalar(out, in0, s1, None, op0, **kw)
            return e.tensor_scalar(out, in0, s1, s2, op0, op1, **kw)
        return self.add(eng, fn, rd, wr)

    def stt(self, out, in0, scalar, in1, op0, op1, eng="dve"):
        rd = [in0, in1]
        if not isinstance(scalar, (int, float)):
            rd.append(scalar)
        return self.add(eng, lambda e: e.scalar_tensor_tensor(out, in0, scalar, in1, op0, op1), rd, [out])

    def copy(self, eng, out, in_):
        if eng == "act":
            return self.add("act", lambda e: e.copy(out, in_), [in_], [out])
        return self.add(eng, lambda e: e.tensor_copy(out, in_), [in_], [out])

    def memset(self, eng, ap, val):
        return self.add(eng, lambda e: e.memset(ap, val), [], [ap])

    def rsqrt(self, out, in_):
        self.act(out, in_, AF.Ln)
        return self.act(out, out, AF.Exp, scale=-0.5)

    def recip(self, out, in_):
        return self.add("dve", lambda e: e.reciprocal(out, in_), [in_], [out])

    def scan(self, out, d0, d1, init, op0, op1):
        rd = [d0, d1]
        if not isinstance(init, (int, float)):
            rd.append(init)
        return self.add("dve", lambda e: e.tensor_tensor_scan(out, d0, d1, init, op0, op1), rd, [out])

    def reduce(self, out, in_, op, axis=AX.X, eng="dve"):
        return self.add(eng, lambda e: e.tensor_reduce(out, in_, axis, op), [in_], [out])

    def emit(self, nc, block, sems, dsems):
        for e in ENGS:
            c = 0
            for op in self.ops[e]:
                if op.signal and not op.is_dma:
                    c += 1
                    op.semval = c
        engobj = {"pe": "tensor", "act": "scalar", "dve": "vector", "pool": "gpsimd", "sp": "sync"}
        outs = self.all_dma_out

        def make(ename):
            ops = self.ops[ename]

            def body(eng):
                for op in ops:
                    for key, dep in op.waits:
                        if dep.is_dma:
                            eng.wait_ge(dsems[dep.slot], dep.slotval)
                        else:
                            eng.wait_ge(sems[dep.eng], dep.semval)
                    ins = op.fn(eng)
                    if op.is_dma:
                        ins.then_inc(dsems[op.slot], 16)
                    elif op.signal:
                        ins.then_inc(sems[op.eng], 1)
                if ename == "sp":
                    for op in outs:
                        eng.wait_ge(dsems[op.slot], op.slotval)
            return body

        for e in ENGS:
            if not self.ops[e] and e != "sp":
                continue
            getattr(block, engobj[e])(make(e))


DEC_SCALE = float(np.exp(-0.5))
SLOPES = [2.0 ** (-8.0 * (h + 1) / 4.0) for h in range(4)]
LAMBDA_INIT = 0.8 - 0.6 * 1.0

PV_ANW, PV_FNW, PV_MU, PV_W0, PV_A0, PV_KK, PV_KA, PV_RK, PV_LNW, PV_LNB, PV_QW, PV_KW = \
    0, 8, 16, 30, 34, 38, 42, 46, 50, 54, 58, 59
PV_CW0, PV_CW1, PV_CW2, PV_CB = 60, 104, 148, 192
NPV = 236
C_BLK1, C_BLKM, C_NH, C_D = 0, 128, 256, 768
NCST = 768 + 2560
CB_ID, CB_SU, CB_SL, CB_UI, CB_I4, CB_RM = 0, 128, 640, 1152, 1664, 2176
CB_KAUG, CB_QAUG = 2688, 3200
NCB = 3712

ARENA_F32 = 52900


class Arena:
    def __init__(self, ar, n):
        self.ar = ar
        self.n = n
        self.p = 0
        self.hi = 0

    def f32(self, n):
        a = self.p
        self.p += n
        assert self.p <= self.n, ("arena overflow", self.p, self.n)
        self.hi = max(self.hi, self.p)
        return self.ar[:, a:a + n]

    def bf(self, n):
        assert n % 2 == 0
        a = self.p
        self.p += n // 2
        assert self.p <= self.n, ("arena overflow", self.p, self.n)
        self.hi = max(self.hi, self.p)
        return self.ar[:, a:a + n // 2].bitcast(BF16)


def build(T, do_a1=True, do_a2=True, do_b=True):
    from contextlib import ExitStack
    NST = T // 512
    NKT = T // 128
    nc = bass.Bass("TRN2", target_bir_lowering=False)

    def din(name, shape):
        return nc.dram_tensor(name, shape, F32, kind="ExternalInput").ap()

    x = din("x", [T, 1024])
    w_in = din("w_in", [1024, 3328])
    w_out = din("w_out", [1024, 1024])
    w_up = din("w_up", [1024, 5632])
    w_dn = din("w_dn", [2816, 1024])
    w_lora = din("w_lora", [128, 512])
    w_gate = din("w_gate", [128, 512])
    pvec = din("pvec", [128, NPV])
    cst = din("cst", [128, NCST])
    cstb = din("cstb", [128, NCB])
    lamv = din("lamv", [128, 256])
    subw = din("subw", [128, 128])
    out = nc.dram_tensor("out", [T, 1024], F32, kind="ExternalOutput").ap()

    S = Sched()
    es = ExitStack()
    with es:
        AR = es.enter_context(nc.sbuf_tensor("AR", [128, ARENA_F32], F32))
        al = Arena(AR, ARENA_F32)
        pbs = [es.enter_context(nc.psum_tensor("pb%d" % i, [128, 512], F32)) for i in range(6)]
        tqs = [es.enter_context(nc.psum_tensor("tq%d" % i, [128, 1024], BF16)) for i in range(2)]
        state = {"pb": 0, "tq": 0}

        def bank():
            b = pbs[state["pb"] % 6]
            state["pb"] += 1
            return b

        def tqhalf():
            k = state["tq"] % 4
            state["tq"] += 1
            return tqs[k // 2][:, (k % 2) * 512:(k % 2) * 512 + 512]

        pv = al.f32(NPV)
        der = al.f32(32)
        omu = der[:, 0:14]
        omka = der[:, 14:18]
        qws = der[:, 18:19]
        kws = der[:, 19:20]
        lam = der[:, 20:21]
        nlam = der[:, 21:22]
        cs = al.f32(768)
        blk1 = cs[:, C_BLK1:C_BLK1 + 128]
        blkm = cs[:, C_BLKM:C_BLKM + 128]
        negh = cs[:, C_NH:C_NH + 512]
        cb = al.bf(NCB)
        ident = cb[:, CB_ID:CB_ID + 128]
        SU4 = cb[:, CB_SU:CB_SU + 512]
        SL4 = cb[:, CB_SL:CB_SL + 512]
        UI4 = cb[:, CB_UI:CB_UI + 512]
        I4 = cb[:, CB_I4:CB_I4 + 512]
        rmask = cb[:, CB_RM:CB_RM + 512]
        kaug = cb[:, CB_KAUG:CB_KAUG + 512].rearrange("p (h k) -> p h k", h=4)
        qaug = cb[:, CB_QAUG:CB_QAUG + 512]
        hT = al.bf(4096).rearrange("p (k t) -> p k t", k=8)
        nrm = al.f32(16)
        NT = {}

        S.dma(pv, pvec)
        S.dma(cs, cst[:, 0:768])
        S.dma(cb, cstb, eng="pool")
        S.ts("dve", omu, pv[:, PV_MU:PV_MU + 14], -1.0, 1.0, ALU.mult, ALU.add)
        S.ts("dve", omka, pv[:, PV_KA:PV_KA + 4], -1.0, 1.0, ALU.mult, ALU.add)
        S.ts("dve", qws, pv[:, PV_QW:PV_QW + 1], 0.125, None, ALU.mult)
        S.ts("dve", kws, pv[:, PV_KW:PV_KW + 1], 1.0, None, ALU.mult)
        base_mark = al.p

        def norm_T_gen(xtile_of_j, wcol0, hT_=None, tqb=None, hn_=None, slack=1, defer=0, hns=None,
                       pool_rsqrt=False):
            hT_ = hT if hT_ is None else hT_
            hns = hns or [NT["hn"] if hn_ is None else hn_]
            junk = NT["junk"]
            tqb = tqb or [tqs[0], tqs[1]]
            wbc = pv[:, wcol0:wcol0 + 8].unsqueeze(2).to_broadcast([128, 8, 128])
            tiles = {0: xtile_of_j(0), 1: xtile_of_j(1)}
            pending = []
            clock = [0]

            def tick():
                clock[0] += 1
                while pending and pending[0][0] <= clock[0]:
                    pending.pop(0)[1]()

            def make_tr(j, hnj):
                def f():
                    tq = tqb[j % len(tqb)]
                    for kc in range(8):
                        S.tr(tq[:, kc * 128:(kc + 1) * 128], hnj[:, kc * 128:(kc + 1) * 128], ident)
                    S.tt("dve", hT_[:, :, j * 128:(j + 1) * 128],
                         tq[:, :].rearrange("p (k t) -> p k t", k=8), wbc, ALU.mult)
                return f
            yield
            yield
            for j in range(4):
                xt = tiles[j]
                hnj = hns[j % len(hns)]
                tick()
                yield
                S.act(junk, xt, AF.Square, accum=nrm[:, j:j + 1])
                tick()
                yield
                S.ts("dve", nrm[:, 4 + j:5 + j], nrm[:, j:j + 1], 1.0 / 1024.0, 1e-6, ALU.mult, ALU.add)
                tick()
                yield
                if pool_rsqrt:
                    S.tt("pool", nrm[:, 8 + j:9 + j], nrm[:, 4 + j:5 + j], negh[:, 0:1], ALU.pow)
                    tick()
                    yield
                else:
                    S.act(nrm[:, 8 + j:9 + j], nrm[:, 4 + j:5 + j], AF.Ln)
                    tick()
                    yield
                    S.act(nrm[:, 8 + j:9 + j], nrm[:, 8 + j:9 + j], AF.Exp, scale=-0.5)
                    tick()
                    yield
                S.ts("dve", hnj, xt, nrm[:, 8 + j:9 + j], None, ALU.mult)
                if j + 2 < 4:
                    tiles[j + 2] = xtile_of_j(j + 2)
                if defer > 0:
                    pending.append((clock[0] + defer, make_tr(j, hnj)))
                    tick()
                    yield
                else:
                    for _ in range(slack):
                        yield
                    make_tr(j, hnj)()
                    yield
            while pending:
                tick()
                yield

        def norm_T(xtile_of_j, wcol0):
            for _ in norm_T_gen(xtile_of_j, wcol0):
                pass

        def proj_fm(wt, col0):
            b = bank()
            for kc in range(8):
                S.mm(b[:], wt[:, kc, col0:col0 + 128], hT[:, kc, :], start=(kc == 0), stop=(kc == 7))
            return b

        pre_a2 = {}
        if do_a1 and do_a2:
            al.p = base_mark
            pre_a2["w2"] = al.bf(8 * 1536).rearrange("p (k n) -> p k n", k=8)
            pre_a2["wob"] = al.bf(4 * 1024).rearrange("p (k n) -> p k n", k=4)
            pre_a2["end"] = al.p
        if do_a1:
            al.p = base_mark
            NT["hn"] = al.bf(1024)
            NT["junk"] = al.bf(1024)
            w1 = al.bf(8 * 1792).rearrange("p (k n) -> p k n", k=8)
            w1_end = al.p
            woa = al.bf(4 * 1024).rearrange("p (k n) -> p k n", k=4)
            wlo = al.bf(512)
            wga = al.bf(512)
            xinA = [al.f32(1024) for _ in range(2)]
            hnA2 = al.bf(1024)
            zraw = [al.f32(514) for _ in range(2)]
            carry2 = [al.f32(16) for _ in range(2)]
            lora12 = al.bf(512)
            lora13 = al.bf(512)
            rT = al.bf(2048).rearrange("p (c t) -> p c t", c=4)
            aT = al.bf(2048).rearrange("p (c t) -> p c t", c=4)
            bT = al.bf(2048).rearrange("p (c t) -> p c t", c=4)
            kT = al.bf(2048).rearrange("p (c t) -> p c t", c=4)
            vT = al.bf(2048).rearrange("p (c t) -> p c t", c=4)
            tokV = al.bf(2048).rearrange("p (j f) -> p j f", j=4)
            tokB = al.bf(2048).rearrange("p (j f) -> p j f", j=4)
            tokK = al.bf(2048).rearrange("p (j f) -> p j f", j=4)
            gsb = al.bf(2048).rearrange("p (c t) -> p c t", c=4)
            bon = al.bf(2048).rearrange("p (c t) -> p c t", c=4)
            ysb = al.f32(2048).rearrange("p (c t) -> p c t", c=4)
            mixA = rT
            CH = [{"M": [al.bf(512) for _ in range(2)], "MT": [al.bf(512) for _ in range(2)],
                   "P": [al.bf(512) for _ in range(2)]} for _ in range(4)]
            AM = [[al.bf(1024) for _ in range(3)] for _ in range(2)]
            Gbf = al.bf(512)
            Ubf = al.bf(512)
            S0T = al.f32(256)
            S0bf = al.bf(256)
            tmpS = al.f32(256)
            Wc = al.f32(16).rearrange("p (c j) -> p c j", c=4)
            tl = [al.f32(512) for _ in range(9)]
            ysb_flat = ysb.rearrange("p c t -> p (c t)")
            tlb = [ysb_flat[:, i * 512:(i + 1) * 512] for i in range(4)] + [al.f32(512) for _ in range(5)]
            TS = [{"tl": tl, "zraw": zraw, "t1": tl[8]},
                  {"tl": tlb, "zraw": [al.f32(514) for _ in range(2)], "t1": tlb[8]}]
            zl = al.f32(512)
            eps12 = al.f32(2)
            S.memset("pool", eps12, 1e-12)
            epsln = al.f32(2)
            S.memset("pool", epsln, 64e-5)
            print("A1 arena words", al.p)

            for kc in range(8):
                S.dma(w1[:, kc, :], w_in[kc * 128:(kc + 1) * 128, 0:1792], eng="pool")
            for kc in range(4):
                S.dma(woa[:, kc, :], w_out[kc * 128:(kc + 1) * 128, :], eng="pool")
            S.dma(wlo, w_lora, eng="pool")
            S.dma(wga, w_gate, eng="pool")
            for c_ in carry2:
                S.memset("pool", c_, 0.0)
            S.memset("pool", S0T, 0.0)
            S.memset("pool", S0bf, 0.0)

            cur_s = [0]

            def lerp_chunk(c, b, zout, ts_=None):
                s_ = cur_s[0]
                cold, cnew = carry2[(s_ + 1) % 2], carry2[s_ % 2]
                mu_ = pv[:, PV_MU + c:PV_MU + c + 1]
                S.act(zout, b[:], AF.Copy, scale=omu[:, c:c + 1])
                S.copy("act", cnew[:, c:c + 1], b[:, 511:512])
                S.stt(zout[:, 1:512], b[:, 0:511], mu_, zout[:, 1:512], ALU.mult, ALU.add)
                S.stt(zout[:, 0:1], cold[:, c:c + 1], mu_, zout[:, 0:1], ALU.mult, ALU.add)

            def gen_normA(s_):
                t0_ = s_ * 512

                def xload(j):
                    xt = xinA[j % 2]
                    S.dma(xt, x[t0_ + j * 128:t0_ + (j + 1) * 128, :])
                    return xt
                yield from norm_T_gen(xload, PV_ANW, defer=6, hns=[NT["hn"], hnA2])

            for _ in gen_normA(0):
                pass
            for s in range(NST):
                t0 = s * 512
                cur_s[0] = s
                bgn = gen_normA(s + 1) if s + 1 < NST else iter(())
                def gen_lora():
                    lerp_chunk(12, proj_fm(w1, 12 * 128), zl)
                    yield
                    S.act(lora12[0:64, :], zl[0:64, :], AF.Tanh)
                    S.copy("pool", lora12[64:128, :], zl[64:128, :])
                    yield
                    lerp_chunk(13, proj_fm(w1, 13 * 128), zl)
                    yield
                    S.act(lora13, zl, AF.Sigmoid)
                    yield
                def gen_pair(c, ts_, delay):
                    for _ in range(delay):
                        yield
                    tl_ = ts_["tl"]
                    zr_, zk_, zv_ = tl_[0], tl_[1], tl_[2]
                    lerp_chunk(c, proj_fm(w1, c * 128), zr_, ts_)
                    yield
                    lerp_chunk(4 + c, proj_fm(w1, (4 + c) * 128), zk_, ts_)
                    yield
                    lerp_chunk(8 + c, proj_fm(w1, (8 + c) * 128), zv_, ts_)
                    yield
                    cc = slice(c * 128, (c + 1) * 128)
                    b = bank()
                    S.mm(b[:], wlo[0:64, cc], lora12[0:64, :])
                    sg = tl_[3]
                    S.act(sg, b[:], AF.Sigmoid, bias=pv[:, PV_W0 + c:PV_W0 + c + 1])
                    b = bank()
                    S.mm(b[:], wlo[64:128, cc], lora12[64:128, :])
                    apm = tl_[8]
                    S.act(apm, b[:], AF.Sigmoid, bias=pv[:, PV_A0 + c:PV_A0 + c + 1])
                    yield
                    Lc = tl_[4]
                    S.scan(Lc, rmask, sg, 0.0, ALU.mult, ALU.add)
                    b = bank()
                    S.mm(b[:], wga[:, cc], lora13)
                    S.copy("act", gsb[:, c, :], b[:])
                    kk = tl_[6]
                    S.ts("dve", kk, zk_, pv[:, PV_KK + c:PV_KK + c + 1], None, ALU.mult)
                    kk2 = tl_[7]
                    S.act(kk2, zk_, AF.Square, scale=pv[:, PV_KK + c:PV_KK + c + 1])
                    b = bank()
                    S.mm(b[:], blk1, kk2)
                    yield
                    eL, enL = tl_[3], tl_[5]
                    S.act(eL, Lc, AF.Exp, scale=-DEC_SCALE)
                    S.act(enL, Lc, AF.Exp, scale=DEC_SCALE)
                    S.copy("pool", Wc[:, c, :], eL[:, 127:512:128])
                    S.act(kk2, b[:], AF.Ln, bias=eps12[:, 0:1])
                    S.act(kk2, kk2, AF.Exp, scale=-0.5)
                    yield
                    t4 = tl_[4]
                    S.ts("dve", t4, apm, pv[:, PV_KA + c:PV_KA + c + 1], omka[:, c:c + 1], ALU.mult, ALU.add)
                    S.tt("pool", t4, zk_, t4, ALU.mult)
                    S.tt("pool", apm, apm, enL, ALU.mult)
                    yield
                    S.tt("dve", kT[:, c, :], t4, enL, ALU.mult)
                    S.tt("dve", rT[:, c, :], zr_, eL, ALU.mult)
                    rk = tl_[1]
                    S.stt(rk, zr_, pv[:, PV_RK + c:PV_RK + c + 1], t4, ALU.mult, ALU.mult)
                    bb_ = bank()
                    S.mm(bb_[:], blk1, rk)
                    S.copy("act", vT[:, c, :], zv_)
                    yield
                    S.tt("dve", kk, kk, kk2, ALU.mult)
                    S.stt(aT[:, c, 1:512], kk[:, 1:512], -1.0, eL[:, 0:511], ALU.mult, ALU.mult)
                    S.ts("dve", aT[:, c, 0:512:128], kk[:, 0:512:128], -1.0, None, ALU.mult)
                    S.tt("dve", bT[:, c, :], kk, apm, ALU.mult)
                    yield
                    S.tt("dve", bon[:, c, :], bb_[:], zv_, ALU.mult)
                    yield

                def seq(*gs):
                    for g in gs:
                        yield from g

                def rr(gens):
                    gens = list(gens)
                    while gens:
                        for g in list(gens):
                            try:
                                next(g)
                            except StopIteration:
                                gens.remove(g)

                if True:
                    rr([seq(gen_pair(0, TS[0], 0), gen_pair(2, TS[0], 0)),
                        seq(gen_pair(1, TS[1], 1), gen_pair(3, TS[1], 0)), gen_lora()])
                if s == NST - 1 and pre_a2:
                    assert pre_a2["end"] <= w1_end, (pre_a2["end"], w1_end)
                    for kc in range(8):
                        S.dma(pre_a2["w2"][:, kc, :], w_in[kc * 128:(kc + 1) * 128, 1792:3328], eng="pool")
                    for kc in range(4):
                        S.dma(pre_a2["wob"][:, kc, :], w_out[512 + kc * 128:512 + (kc + 1) * 128, :], eng="pool")
                    pre_a2["done"] = True
                def gen_tok():
                    for j in range(4):
                        for (src_, dst) in ((vT, tokV), (bT, tokB), (kT, tokK)):
                            tq = tqhalf()
                            for c in range(4):
                                S.tr(tq[:, c * 128:(c + 1) * 128], src_[:, c, j * 128:(j + 1) * 128], ident)
                            S.copy("act" if dst is tokB else "dve", dst[:, j, :], tq)
                            yield
                def gen_chain(j, gi):
                    tj = slice(j * 128, (j + 1) * 128)
                    ch = CH[(j % 2) * 2 + gi]
                    Mb_, MTb_, Pb_ = ch["M"], ch["MT"], ch["P"]
                    AkT_, RbT_, RkT_ = AM[j % 2]
                    r0 = gi * 64
                    rows = slice(r0, r0 + 64)
                    gsl = slice(gi * 512, (gi + 1) * 512)
                    for typ in range(5):
                        b = bank()
                        for hh in range(4):
                            A_, B_, K_, R_ = aT[rows, hh, tj], bT[rows, hh, tj], kT[rows, hh, tj], rT[rows, hh, tj]
                            lhsT, rhs = ((B_, A_), (A_, B_), (K_, A_), (B_, R_), (K_, R_))[typ]
                            S.mm(b[:, hh * 128:(hh + 1) * 128], lhsT, rhs)
                        if typ == 0:
                            S.tt("dve", Mb_[0], b[:], SU4, ALU.mult)
                            S.tt("pool", Pb_[0], Mb_[0], I4, ALU.add)
                        elif typ == 1:
                            S.tt("dve", MTb_[0], b[:], SL4, ALU.mult)
                        elif typ == 2:
                            S.tt("dve", AkT_[:, gsl], b[:], SU4, ALU.mult)
                        elif typ == 3:
                            S.tt("dve", RbT_[:, gsl], b[:], UI4, ALU.mult)
                        else:
                            S.tt("dve", RkT_[:, gsl], b[:], UI4, ALU.mult)
                        yield
                    for l in range(1, 7):
                        sr, ds = (l - 1) % 2, l % 2
                        if l < 6:
                            b = bank()
                            for hh in range(4):
                                q = slice(hh * 128, (hh + 1) * 128)
                                S.mm(b[:, q], MTb_[sr][:, q], Mb_[sr][:, q])
                            S.copy("act", Mb_[ds], b[:])
                        b = bank()
                        for hh in range(4):
                            q = slice(hh * 128, (hh + 1) * 128)
                            S.mm(b[:, q], Mb_[sr][:, q], MTb_[sr][:, q])
                        S.copy("act", MTb_[ds], b[:])
                        yield
                        b = bank()
                        for hh in range(4):
                            q = slice(hh * 128, (hh + 1) * 128)
                            S.mm(b[:, q], MTb_[ds][:, q], Pb_[sr][:, q])
                        S.tt("dve", Pb_[ds], b[:], Pb_[sr], ALU.add)
                        yield

                def gen_rec(j):
                    tj = slice(j * 128, (j + 1) * 128)
                    AkT_, RbT_, RkT_ = AM[j % 2]
                    TT = [CH[(j % 2) * 2 + gi]["P"][0] for gi in range(2)]
                    for gi in range(2):
                        r0 = gi * 64
                        rows = slice(r0, r0 + 64)
                        bG = bank()
                        for hh in range(4):
                            fs = slice(hh * 128 + r0, hh * 128 + r0 + 64)
                            o = bG[:, hh * 64:(hh + 1) * 64]
                            S.mm(o, aT[rows, hh, tj], S0bf[rows, hh * 64:(hh + 1) * 64], start=True, stop=False)
                            S.mm(o, AkT_[:, gi * 512 + hh * 128:gi * 512 + (hh + 1) * 128], tokV[:, j, fs],
                                 start=False, stop=True)
                        S.copy("act" if gi == 0 else "dve", Gbf[:, gi * 256:(gi + 1) * 256], bG[:, 0:256])
                    yield
                    bU = bank()
                    for gi in range(2):
                        for hh in range(4):
                            sl_ = slice(gi * 256 + hh * 64, gi * 256 + (hh + 1) * 64)
                            S.mm(bU[:, sl_], TT[gi][:, hh * 128:(hh + 1) * 128], Gbf[:, sl_])
                    S.copy("act", Ubf, bU[:])
                    yield
                    bS = bank()
                    for gi in range(2):
                        r0 = gi * 64
                        rows = slice(r0, r0 + 64)
                        for hh in range(4):
                            fs = slice(hh * 128 + r0, hh * 128 + r0 + 64)
                            sl_ = slice(gi * 256 + hh * 64, gi * 256 + (hh + 1) * 64)
                            o = bS[rows, hh * 64:(hh + 1) * 64]
                            S.mm(o, tokB[:, j, fs], Ubf[:, sl_], start=True, stop=False)
                            S.mm(o, tokK[:, j, fs], tokV[:, j, fs], start=False, stop=True)
                    for gi in range(2):
                        r0 = gi * 64
                        rows = slice(r0, r0 + 64)
                        bY = bank()
                        for hh in range(4):
                            fs = slice(hh * 128 + r0, hh * 128 + r0 + 64)
                            sl_ = slice(gi * 256 + hh * 64, gi * 256 + (hh + 1) * 64)
                            o = bY[rows, hh * 128:(hh + 1) * 128]
                            S.mm(o, S0bf[rows, hh * 64:(hh + 1) * 64], rT[rows, hh, tj], start=True, stop=False)
                            S.mm(o, Ubf[:, sl_], RbT_[:, gi * 512 + hh * 128:gi * 512 + (hh + 1) * 128],
                                 start=False, stop=False)
                            S.mm(o, tokV[:, j, fs], RkT_[:, gi * 512 + hh * 128:gi * 512 + (hh + 1) * 128],
                                 start=False, stop=True)
                        S.copy("act", ysb[rows, :, tj], bY[rows, :].rearrange("p (c t) -> p c t", c=4))
                    S.tt("dve", tmpS, bS[:, 0:256], S0T, ALU.add)
                    S.tt("dve", S0T.rearrange("p (c v) -> p c v", c=4),
                         tmpS.rearrange("p (c v) -> p c v", c=4),
                         Wc[:, :, j:j + 1].to_broadcast([128, 4, 64]), ALU.mult)
                    S.copy("pool", S0bf, S0T)
                    yield

                def seq(*gs):
                    for g in gs:
                        yield from g

                def rr(gens, bg=None):
                    gens = list(gens)
                    while gens:
                        for g in list(gens):
                            try:
                                next(g)
                            except StopIteration:
                                gens.remove(g)
                        if bg is not None:
                            next(bg, None)

                def dly(g, n):
                    for _ in range(n):
                        yield
                    yield from g

                rr([gen_chain(0, 0), dly(gen_chain(0, 1), 1), dly(gen_chain(1, 0), 2), dly(gen_chain(1, 1), 3),
                    gen_tok()], bg=bgn)
                rr([gen_rec(0)], bg=bgn)
                rr([gen_rec(1), gen_chain(2, 0), dly(gen_chain(2, 1), 1)], bg=bgn)
                rr([gen_rec(2), gen_chain(3, 0), dly(gen_chain(3, 1), 1)], bg=bgn)
                rr([gen_rec(3)], bg=bgn)
                for _ in bgn:
                    pass
                def gen_gn(c, d, dsq, delay):
                    for _ in range(delay):
                        yield
                    b1 = bank()
                    S.mm(b1[:], blkm, ysb[:, c, :])
                    yield
                    S.tt("dve", d, ysb[:, c, :], b1[:], ALU.subtract)
                    yield
                    S.tt("pool", dsq, d, d, ALU.mult)
                    yield
                    b2 = bank()
                    S.mm(b2[:], blkm, dsq)
                    yield
                    S.act(dsq, b2[:], AF.Ln, bias=epsln[:, 0:1])
                    yield
                    S.act(dsq, dsq, AF.Exp, scale=-0.5)
                    yield
                    S.tt("dve", d, d, dsq, ALU.mult)
                    S.ts("dve", d, d, pv[:, PV_LNW + c:PV_LNW + c + 1], pv[:, PV_LNB + c:PV_LNB + c + 1],
                         ALU.mult, ALU.add)
                    yield
                    S.tt("pool", d, d, bon[:, c, :], ALU.add)
                    yield
                    S.tt("dve", mixA[:, c, :], d, gsb[:, c, :], ALU.mult)
                    yield

                if True:
                    rr([seq(gen_gn(0, tl[0], tl[1], 0), gen_gn(2, tl[0], tl[1], 0)),
                        seq(gen_gn(1, tlb[4], tlb[5], 2), gen_gn(3, tlb[4], tlb[5], 0))])
                xt_ = {}
                for j in range(2):
                    xt_[j] = xinA[j % 2]
                    S.dma(xt_[j], x[t0 + j * 128:t0 + (j + 1) * 128, :])
                for j in range(4):
                    tj = slice(j * 128, (j + 1) * 128)
                    xt = xt_[j]
                    for half in range(2):
                        b = bank()
                        hsl = slice(half * 512, (half + 1) * 512)
                        for c in range(4):
                            S.mm(b[:], mixA[:, c, tj], woa[:, c, hsl], start=(c == 0), stop=(c == 3))
                        S.tt("dve", xt[:, hsl], b[:], xt[:, hsl], ALU.add)
                    S.dma(out[t0 + j * 128:t0 + (j + 1) * 128, :], xt, is_output=True)
                    if j + 2 < 4:
                        xt_[j + 2] = xinA[j % 2]
                        S.dma(xt_[j + 2], x[t0 + (j + 2) * 128:t0 + (j + 3) * 128, :])
        src_a2 = out if do_a1 else x

        if do_a2:
            al.p = base_mark
            w2 = al.bf(8 * 1536).rearrange("p (k n) -> p k n", k=8)
            wob = al.bf(4 * 1024).rearrange("p (k n) -> p k n", k=4)
            KA = [al.bf(4 * T).rearrange("p (h t) -> p h t", h=4) for _ in range(2)]
            Vat = al.bf(NKT * 4 * 130).rearrange("p (k h d) -> p k h d", k=NKT, h=4)
            Dt = al.f32(128)
            rs2 = [al.f32(512) for _ in range(2)]
            hn3 = al.bf(1024)
            subr = al.f32(128)
            lmv = al.f32(256)
            xin = [al.f32(1024) for _ in range(2)]
            QA = [[al.bf(2048).rearrange("p (h t) -> p h t", h=4) for _ in range(2)] for _ in range(2)]
            mixB = al.bf(2048).rearrange("p (h t) -> p h t", h=4)
            pT = [al.bf(512) for _ in range(4)]
            tn = [al.f32(512).rearrange("p (q d) -> p q d", q=4) for _ in range(2)]
            etmp = [al.f32(512) for _ in range(2)]
            tl2 = [al.f32(512) for _ in range(2)]
            Ot = al.f32(512).rearrange("p (q d) -> p q d", q=4)
            Obf = al.bf(512).rearrange("p (q d) -> p q d", q=4)
            sm = al.f32(32)
            hn2 = al.bf(1024)
            NT["hn"] = hn2
            NT["junk"] = tl2[1].bitcast(BF16)
            eps6 = al.f32(2)
            S.memset("pool", eps6, 1e-6)
            print("A2 arena words", al.p)

            if not pre_a2.get("done"):
                for kc in range(8):
                    S.dma(w2[:, kc, :], w_in[kc * 128:(kc + 1) * 128, 1792:3328], eng="pool")
                for kc in range(4):
                    S.dma(wob[:, kc, :], w_out[512 + kc * 128:512 + (kc + 1) * 128, :], eng="pool")
            S.dma(Dt, cst[:, C_D:C_D + 128])
            for m in range(2):
                ar0 = 64 * (1 - m)
                S.memset("dve", KA[m].rearrange("p h t -> p (h t)"), 0.0)
                for qq in range(2):
                    S.memset("dve", QA[qq][m].rearrange("p h t -> p (h t)"), 0.0)
                for kt in range(NKT):
                    S.copy("pool", KA[m][ar0:ar0 + 3, :, kt * 128:(kt + 1) * 128], kaug[ar0:ar0 + 3, :, :])
                for qq in range(2):
                    for h in range(4):
                        S.copy("pool", QA[qq][m][ar0:ar0 + 3, h, :], qaug[ar0:ar0 + 3, :])
            S.dma(subr, subw)
            S.dma(lmv, lamv)
            S.ts("dve", subr, subr, 1.0 - LAMBDA_INIT, None, ALU.mult)
            S.tt("dve", tl2[0][:, 0:64], lmv[:, 0:64], lmv[:, 64:128], ALU.mult)
            S.tt("dve", tl2[0][:, 64:128], lmv[:, 128:192], lmv[:, 192:256], ALU.mult)
            S.reduce(sm[:, 0:2], tl2[0][:, 0:128].rearrange("p (a d) -> p a d", a=2), ALU.add)
            S.act(sm[:, 2:4], sm[:, 0:2], AF.Exp)
            S.tt("dve", sm[:, 4:5], sm[:, 2:3], sm[:, 3:4], ALU.subtract)
            S.ts("dve", nlam, sm[:, 4:5], -1.0, -LAMBDA_INIT, ALU.mult, ALU.add)
            for kt in range(NKT):
                S.memset("pool", Vat[:, kt, :, 128:129], 1.0)

            gb_state = {"i": 0}
            gbanks = [pbs[5], tqs[0][:].bitcast(F32)]

            def gbank():
                gb_state["i"] += 1
                return gbanks[gb_state["i"] % 2]

            def gen_pre(s):
                t0 = s * 512
                qa = QA[s % 2]

                def xload(j):
                    xt = xin[j % 2]
                    S.dma(xt, x[t0 + j * 128:t0 + (j + 1) * 128, :])
                    return xt
                yield from norm_T_gen(xload, PV_ANW, tqb=[tqs[1]], hns=[hn2, hn3], defer=3)

                def gen_qk(chunks, raw, sq, delay):
                    for _ in range(delay):
                        yield
                    for (h, which) in chunks:
                        b = gbank()
                        col0 = which * 512 + h * 128
                        for kc in range(8):
                            S.mm(b[:, :], w2[:, kc, col0:col0 + 128], hT[:, kc, :], start=(kc == 0), stop=(kc == 7))
                        yield
                        S.copy("dve", raw, b[:, :])
                        S.act(sq, b[:, :], AF.Square)
                        yield
                        b2 = gbank()
                        S.mm(b2[:, :], blkm, sq)
                        yield
                        S.act(sq, b2[:, :], AF.Ln, bias=eps6[:, 0:1])
                        yield
                        S.act(sq, sq, AF.Exp, scale=-0.5)
                        yield
                        for m in range(2):
                            rr_ = slice(m * 64, (m + 1) * 64)
                            if which == 0:
                                S.stt(qa[m][rr_, h, :], raw[rr_, :], qws[rr_, :], sq[rr_, :], ALU.mult, ALU.mult)
                            else:
                                S.stt(KA[m][rr_, h, t0:t0 + 512], raw[rr_, :], kws[rr_, :], sq[rr_, :],
                                      ALU.mult, ALU.mult)
                        yield
                chunks = [(h, which) for h in range(4) for which in range(2)]
                ga = gen_qk(chunks[0::2], tl2[0], tl2[1], 0)
                gb = gen_qk(chunks[1::2], rs2[0], rs2[1], 3)
                live = [ga, gb]
                while live:
                    for g in list(live):
                        try:
                            next(g)
                        except StopIteration:
                            live.remove(g)
                    yield
                for j in range(4):
                    b = gbank()
                    for kc in range(8):
                        S.mm(b[:, :], hT[:, kc, j * 128:(j + 1) * 128], w2[:, kc, 1024:1536],
                             start=(kc == 0), stop=(kc == 7))
                    yield
                    S.copy("dve", Vat[:, 4 * s + j, :, 0:128], b[:, :].rearrange("p (h d) -> p h d", h=4))
                    yield

            def gen_post(s):
                t0 = s * 512
                for j in range(4):
                    tj = slice(j * 128, (j + 1) * 128)
                    xt = xin[j % 2]
                    S.dma(xt, src_a2[t0 + j * 128:t0 + (j + 1) * 128, :])
                    bs_ = []
                    for half in range(2):
                        b = gbank()
                        bs_.append(b)
                        hsl = slice(half * 512, (half + 1) * 512)
                        for c in range(4):
                            S.mm(b[:, :], mixB[:, c, tj], wob[:, c, hsl], start=(c == 0), stop=(c == 3))
                    for half in range(2):
                        hsl = slice(half * 512, (half + 1) * 512)
                        S.tt("dve", xt[:, hsl], bs_[half][:, :], xt[:, hsl], ALU.add)
                    S.dma(out[t0 + j * 128:t0 + (j + 1) * 128, :], xt, is_output=True)
                    yield

            def gen_loop(s):
                qa = QA[s % 2]
                nkt = 4 * s + 4
                steps = [(h, m, kt) for h in range(4) for m in range(2) for kt in range(nkt)]
                scb = [pbs[2], pbs[3], pbs[4]]
                accp = [pbs[0], pbs[1]]
                LA = 2
                started = {}

                def emit_score(i):
                    h, m, kt = steps[i]
                    jd = kt - 4 * s
                    q0 = 128 * jd if jd > 0 else 0
                    sc = scb[i % 3]
                    r0 = m * 64
                    rows = slice(r0, r0 + 64)
                    S.mm(sc[:, q0:512], KA[m][:, h, kt * 128:(kt + 1) * 128], qa[m][:, h, q0:512])

                def emit_rest(i):
                    h, m, kt = steps[i]
                    u = h * 2 + m
                    jd = kt - 4 * s
                    q0 = 128 * jd if jd > 0 else 0
                    sc = scb[i % 3]
                    pt = pT[i % 4]
                    cimm = -SLOPES[h] * (512 * s - 128 * kt - 128)
                    if jd >= 0:
                        blk = sc[:, q0:q0 + 128]
                        S.stt(blk, Dt[:, 0:128], float(SLOPES[h]), blk, ALU.mult, ALU.add)
                    S.act(pt[:, q0:512], sc[:, q0:512], AF.Exp, bias=float(cimm))
                    for qj in range(max(jd, 0), 4):
                        ab = accp[qj // 2]
                        o = ab[:, (qj % 2) * 130:(qj % 2) * 130 + 129]
                        key = (u, qj // 2)
                        st = key not in started
                        started[key] = True
                        S.mm(o, pt[:, qj * 128:(qj + 1) * 128], Vat[:, kt, h, 0:129],
                             start=st, stop=(kt == 4 * s + qj), skip=True)
                    if kt == nkt - 1:
                        for bb in range(2):
                            S.recip(sm[:, 8 + bb * 2:8 + bb * 2 + 2], accp[bb][:, 128:260:130])
                            S.tt("dve", tn[m][:, 2 * bb:2 * bb + 2, :],
                                 accp[bb][:, 0:260].rearrange("p (q d) -> p q d", q=2)[:, :, 0:128],
                                 sm[:, 8 + bb * 2:8 + bb * 2 + 2].unsqueeze(2).to_broadcast([128, 2, 128]), ALU.mult)
                        if m == 1:
                            Of = Ot.rearrange("p q d -> p (q d)")
                            S.stt(Of, tn[1].rearrange("p q d -> p (q d)"), nlam,
                                  tn[0].rearrange("p q d -> p (q d)"), ALU.mult, ALU.add)
                            S.tt("pool", tn[0].rearrange("p q d -> p (q d)"), Of, Of, ALU.mult)
                            S.reduce(sm[:, 16:20], tn[0], ALU.add)
                            S.ts("dve", sm[:, 20:24], sm[:, 16:20], 1.0 / 128.0, 1e-6, ALU.mult, ALU.add)

                            def stage_b():
                                S.rsqrt(sm[:, 24:28], sm[:, 20:24])

                            def stage_c(h=h):
                                S.tt("dve", Ot, Ot, sm[:, 24:28].unsqueeze(2).to_broadcast([128, 4, 128]), ALU.mult)
                                S.tt("pool", Obf, Ot, subr.unsqueeze(1).to_broadcast([128, 4, 128]), ALU.mult)
                                tq = tqs[1][:, 0:512]
                                for qj in range(4):
                                    S.tr(tq[:, qj * 128:(qj + 1) * 128], Obf[:, qj, :], ident)
                                S.copy("dve", mixB[:, h, :], tq)
                            deferred.append((i + LA + dB, stage_b))
                            deferred.append((i + LA + dC, stage_c))

                dB, dC = (3, 5) if nkt < 8 else (8, 12)
                deferred = []

                def run_deferred(now):
                    while deferred and deferred[0][0] <= now:
                        deferred.pop(0)[1]()

                for i in range(len(steps) + LA):
                    if i < len(steps):
                        emit_score(i)
                    if i >= LA:
                        emit_rest(i - LA)
                    run_deferred(i)
                    yield
                run_deferred(1 << 30)

            def seq2(*gs):
                for g in gs:
                    yield from g

            def rr2(gens):
                gens = [g if isinstance(g, tuple) else (g, 1) for g in gens]
                while gens:
                    for g in list(gens):
                        try:
                            for _ in range(g[1]):
                                next(g[0])
                        except StopIteration:
                            gens.remove(g)

            rr2([gen_pre(0)])
            for s in range(NST):
                others = []
                if s > 0:
                    others.append(gen_post(s - 1))
                if s + 1 < NST:
                    others.append(gen_pre(s + 1))
                nsteps = 8 * (4 * s + 4)
                rr2([(gen_loop(s), 3 if s < 6 else 4), (seq2(*others), 1)])
            rr2([gen_post(NST - 1)])
        src_b = out if (do_a1 or do_a2) else x

        if do_b:
            al.p = base_mark
            NT["hn"] = al.bf(1024)
            NT["junk"] = al.bf(1024)
            wup = al.bf(8 * 5632).rearrange("p (k n) -> p k n", k=8)
            wdn = al.bf(22 * 1024).rearrange("p (k n) -> p k n", k=22)
            gT = al.bf(22 * 512).rearrange("p (c t) -> p c t", c=22)
            xin = [al.f32(1024) for _ in range(2)]
            ccab = [al.f32(88).rearrange("p (c k) -> p c k", c=44) for _ in range(2)]
            tb = [al.f32(512) for _ in range(3)]
            hT2 = al.bf(4096).rearrange("p (k t) -> p k t", k=8)
            hnB2 = al.bf(1024)
            print("B arena words", al.p)
            for kc in range(8):
                for part in range(2):
                    S.dma(wup[:, kc, part * 2816:(part + 1) * 2816],
                          w_up[kc * 128:(kc + 1) * 128, part * 2816:(part + 1) * 2816], eng="pool")
            for kc in range(22):
                S.dma(wdn[:, kc, :], w_dn[kc * 128:(kc + 1) * 128, :], eng="pool")
            for cc_ in ccab:
                S.memset("pool", cc_.rearrange("p c k -> p (c k)"), 0.0)

            hTb = [hT, hT2]

            def gen_normB(s):
                t0 = s * 512

                def xload(j):
                    xt = xin[j % 2]
                    S.dma(xt, src_b[t0 + j * 128:t0 + (j + 1) * 128, :])
                    return xt
                yield from norm_T_gen(xload, PV_FNW, hT_=hTb[s % 2], defer=9, hns=[NT["hn"], hnB2],
                                      pool_rsqrt=True)

            def gen_ffn(s):
                hcur = hTb[s % 2]
                n = 0
                for cg in range(22):
                    res = []
                    for ch in (cg, 22 + cg):
                        b = bank()
                        for kc in range(8):
                            S.mm(b[:], wup[:, kc, ch * 128:(ch + 1) * 128], hcur[:, kc, :],
                                 start=(kc == 0), stop=(kc == 7))
                        t1 = tb[n % 2]
                        n += 1
                        cold, cnew = ccab[(s + 1) % 2], ccab[s % 2]
                        w0_ = pv[:, PV_CW0 + ch:PV_CW0 + ch + 1]
                        w1_ = pv[:, PV_CW1 + ch:PV_CW1 + ch + 1]
                        S.act(t1, b[:], AF.Identity, scale=pv[:, PV_CW2 + ch:PV_CW2 + ch + 1],
                              bias=pv[:, PV_CB + ch:PV_CB + ch + 1])
                        S.copy("act", cnew[:, ch, :], b[:, 510:512])
                        S.stt(t1[:, 1:512], b[:, 0:511], w1_, t1[:, 1:512], ALU.mult, ALU.add)
                        S.stt(t1[:, 0:1], cold[:, ch, 1:2], w1_, t1[:, 0:1], ALU.mult, ALU.add)
                        S.stt(t1[:, 2:512], b[:, 0:510], w0_, t1[:, 2:512], ALU.mult, ALU.add)
                        S.stt(t1[:, 0:2], cold[:, ch, 0:2], w0_, t1[:, 0:2], ALU.mult, ALU.add)
                        res.append(t1)
                        yield
                    sg = tb[2]
                    S.act(sg, res[0], AF.Silu)
                    S.tt("pool", gT[:, cg, :], sg, res[1], ALU.mult)

            def postB(s):
                t0 = s * 512
                tiles = {}
                for j in range(2):
                    tiles[j] = xin[j % 2]
                    S.dma(tiles[j], src_b[t0 + j * 128:t0 + (j + 1) * 128, :])
                for j in range(4):
                    tj = slice(j * 128, (j + 1) * 128)
                    xt = tiles[j]
                    for half in range(2):
                        b = bank()
                        hsl = slice(half * 512, (half + 1) * 512)
                        for cg in range(22):
                            S.mm(b[:], gT[:, cg, tj], wdn[:, cg, hsl], start=(cg == 0), stop=(cg == 21))
                        S.tt("dve", xt[:, hsl], b[:], xt[:, hsl], ALU.add)
                    S.dma(out[t0 + j * 128:t0 + (j + 1) * 128, :], xt, is_output=True)
                    if j + 2 < 4:
                        tiles[j + 2] = xin[j % 2]
                        S.dma(tiles[j + 2], src_b[t0 + (j + 2) * 128:t0 + (j + 3) * 128, :])

            def rrB(gens):
                gens = list(gens)
                while gens:
                    for g in list(gens):
                        try:
                            next(g)
                        except StopIteration:
                            gens.remove(g)

            rrB([gen_normB(0)])
            for s in range(NST):
                gs = [gen_ffn(s)]
                if s + 1 < NST:
                    gs.append(gen_normB(s + 1))
                rrB(gs)
                postB(s)

        print("arena hi words", al.hi, "of", ARENA_F32, "n ops", {e: len(S.ops[e]) for e in ENGS})
        sems = {e: es.enter_context(nc.semaphore("s_" + e)) for e in ENGS}
        dsems = [es.enter_context(nc.semaphore("d%d" % i)) for i in range(S.n_slots)]
        with nc.Block() as block:
            S.emit(nc, block, sems, dsems)
    return nc


def _cols(v, nchunk):
    return np.ascontiguousarray(np.asarray(v, np.float32).reshape(nchunk, 128).T)


def host_consts():
    p = np.arange(128)
    blk1 = (p[:, None] // 64 == p[None, :] // 64).astype(np.float32)
    cst = np.zeros((128, NCST), np.float32)
    cst[:, C_BLK1:C_BLK1 + 128] = blk1
    cst[:, C_BLKM:C_BLKM + 128] = blk1 / 64.0
    cst[:, C_NH:C_NH + 512] = -0.5
    xq = np.arange(640)
    allowed = (p[:, None] // 64) <= (xq[None, :] // 64)
    d = 2.0 * np.minimum(xq[None, :] - p[:, None], 0).astype(np.float32)
    cst[:, C_D:C_D + 640] = np.where(allowed, d, -1.0e6)
    cb = np.zeros((128, NCB), np.float32)
    col = np.arange(128)
    eye = np.eye(128, dtype=np.float32)
    su = (col[None, :] > p[:, None]).astype(np.float32)
    sl = (col[None, :] < p[:, None]).astype(np.float32)
    ui = (col[None, :] >= p[:, None]).astype(np.float32)
    cb[:, CB_ID:CB_ID + 128] = eye
    cb[:, CB_SU:CB_SU + 512] = np.tile(su, (1, 4))
    cb[:, CB_SL:CB_SL + 512] = np.tile(sl, (1, 4))
    cb[:, CB_UI:CB_UI + 512] = np.tile(ui, (1, 4))
    cb[:, CB_I4:CB_I4 + 512] = np.tile(eye, (1, 4))
    rm = np.ones(512, np.float32)
    rm[0::128] = 0.0
    cb[:, CB_RM:CB_RM + 512] = rm[None, :]
    ql = np.arange(512)
    for r0 in (0, 64):
        for h in range(4):
            cb[r0 + 0, CB_KAUG + h * 128:CB_KAUG + (h + 1) * 128] = SLOPES[h]
            cb[r0 + 1, CB_KAUG + h * 128:CB_KAUG + (h + 1) * 128] = SLOPES[h]
            cb[r0 + 2, CB_KAUG + h * 128:CB_KAUG + (h + 1) * 128] = SLOPES[h] * np.arange(128)
        cb[r0 + 0, CB_QAUG:CB_QAUG + 512] = -(128.0 + 128.0 * (ql // 128))
        cb[r0 + 1, CB_QAUG:CB_QAUG + 512] = -(ql % 128).astype(np.float32)
        cb[r0 + 2, CB_QAUG:CB_QAUG + 512] = 1.0
    return cst, cb


def host_params(inp):
    g = lambda k: np.asarray(inp[k], np.float32)[0]
    pvec = np.zeros((128, NPV), np.float32)
    pvec[:, PV_ANW:PV_ANW + 8] = _cols(g("attn_norm_w"), 8)
    pvec[:, PV_FNW:PV_FNW + 8] = _cols(g("ffn_norm_w"), 8)
    pvec[:, PV_MU:PV_MU + 14] = _cols(g("mu_shift"), 14)
    pvec[:, PV_W0:PV_W0 + 4] = _cols(g("w0"), 4)
    pvec[:, PV_A0:PV_A0 + 4] = _cols(g("a0"), 4)
    pvec[:, PV_KK:PV_KK + 4] = _cols(g("k_k"), 4)
    pvec[:, PV_KA:PV_KA + 4] = _cols(g("k_a"), 4)
    pvec[:, PV_RK:PV_RK + 4] = _cols(g("r_k").reshape(-1), 4)
    pvec[:, PV_LNW:PV_LNW + 4] = _cols(g("ln_x_w"), 4)
    pvec[:, PV_LNB:PV_LNB + 4] = _cols(g("ln_x_b"), 4)
    pvec[:, PV_QW] = np.tile(g("q_norm_w"), 2)
    pvec[:, PV_KW] = np.tile(g("k_norm_w"), 2)
    cw = g("ffn_conv_w")
    pvec[:, PV_CW0:PV_CW0 + 44] = _cols(cw[0], 44)
    pvec[:, PV_CW1:PV_CW1 + 44] = _cols(cw[1], 44)
    pvec[:, PV_CW2:PV_CW2 + 44] = _cols(cw[2], 44)
    pvec[:, PV_CB:PV_CB + 44] = _cols(g("ffn_conv_b"), 44)
    lamv = np.concatenate([g("lambda_q1"), g("lambda_k1"), g("lambda_q2"), g("lambda_k2")])
    lamv = np.ascontiguousarray(np.broadcast_to(lamv[None, :], (128, 256)))
    subw = np.ascontiguousarray(np.broadcast_to(g("subln_w")[None, :], (128, 128)))
    w_lora = np.ascontiguousarray(np.concatenate([g("w_decay_up"), g("w_aaa_up")], axis=0))
    cst, cb = host_consts()
    return {
        "w_in": np.ascontiguousarray(g("w_in")), "w_out": np.ascontiguousarray(g("w_out")),
        "w_up": np.ascontiguousarray(g("w_ffn_up")), "w_dn": np.ascontiguousarray(g("w_ffn_down")),
        "w_lora": w_lora, "w_gate": np.ascontiguousarray(g("w_gate_up")),
        "pvec": pvec, "cst": cst, "cstb": cb, "lamv": lamv, "subw": subw,
    }


_NC_CACHE = {}


def run(inputs, T, n_cores, **flags):
    key = (T, tuple(sorted(flags.items())))
    if key not in _NC_CACHE:
        _NC_CACHE[key] = build(T, **flags)
    nc = _NC_CACHE[key]
    shared = host_params(inputs)
    xfull = np.asarray(inputs["x"], np.float32)
    in_maps = []
    for b in range(n_cores):
        m = dict(shared)
        m["x"] = np.ascontiguousarray(xfull[b, :T])
        in_maps.append(m)
    res = run_bass_kernel_spmd(nc, in_maps, core_ids=list(range(n_cores)))
    return np.stack([np.asarray(r["out"]) for r in res.results], axis=0)


def kernel(**inputs):
    return run(inputs, 4096, 8).astype(np.float32)
```

```python
import numpy as np
import concourse.bass as bass
import concourse.mybir as mybir
from concourse.bass_utils import run_bass_kernel_spmd

F32 = mybir.dt.float32
BF16 = mybir.dt.bfloat16
ALU = mybir.AluOpType
AF = mybir.ActivationFunctionType
AX = mybir.AxisListType

ENGS = ("pe", "act", "dve", "pool", "sp")


class Op:
    __slots__ = ("eng", "seq", "fn", "waits", "signal", "is_dma", "slot", "slotval", "semval", "gid")

    def __init__(self, eng, seq, fn, is_dma):
        self.eng = eng
        self.seq = seq
        self.fn = fn
        self.waits = []
        self.signal = False
        self.is_dma = is_dma
        self.slot = None
        self.slotval = None
        self.semval = None


def _footprint(ap):
    t = ap.tensor
    name = t.name
    space = str(ap.space)
    dims = ap.ap
    off = int(ap.offset)
    isz = mybir.dt.size(ap.dtype)
    if space == "PSUM":
        return name, True, (0, 128, 0, 1 << 30)
    if space == "SB":
        psz = dims[0][0]
        if psz <= 0:
            psz = 1
            for s in t.shape[1:]:
                psz *= int(s)
        p0 = off // psz
        f0 = off % psz
        pc = dims[0][1]
        ext = 1
        for st, cnt in dims[1:]:
            ext += (cnt - 1) * abs(st)
        return name, False, (p0, p0 + pc, f0 * isz, (f0 + ext) * isz)
    ext = 1
    for st, cnt in dims:
        ext += (cnt - 1) * abs(st)
    return name, False, (0, 1, off * isz, (off + ext) * isz)


class Sched:
    def __init__(self, n_slots=40):
        self.ops = {e: [] for e in ENGS}
        self.known = {e: {} for e in ENGS}
        self.recs = {}
        self.n_slots = n_slots
        self.n_dma = 0
        self.n_dma_q = {}
        self.slot_last = [None] * n_slots
        self.all_dma_out = []

    def _need(self, op, dep):
        if dep is None or dep is op:
            return
        if dep.is_dma:
            key = ("dma", dep.slot)
            val = dep.slotval
        else:
            if dep.eng == op.eng and op.eng == "pe":
                return
            key = dep.eng
            val = dep.seq
        kn = self.known[op.eng]
        if kn.get(key, -1) >= val:
            return
        kn[key] = val
        dep.signal = True
        for i, (k, d) in enumerate(op.waits):
            if k == key:
                op.waits[i] = (key, dep)
                return
        op.waits.append((key, dep))

    def _access(self, op, ap, is_write):
        name, is_psum, fp = _footprint(ap)
        recs = self.recs.setdefault(name, [])
        p0, p1, f0, f1 = fp
        keep = []
        for r in recs:
            rp0, rp1, rf0, rf1, wr, rds = r
            if rp0 < p1 and p0 < rp1 and rf0 < f1 and f0 < rf1:
                if wr is not None:
                    self._need(op, wr)
                if is_write or is_psum:
                    for rd in rds:
                        self._need(op, rd)
                if is_write and rp0 >= p0 and rp1 <= p1 and rf0 >= f0 and rf1 <= f1:
                    continue
            keep.append(r)
        recs[:] = keep
        if is_write:
            recs.append([p0, p1, f0, f1, op, []])
        else:
            for r in recs:
                if r[0] == p0 and r[1] == p1 and r[2] == f0 and r[3] == f1:
                    rds = r[5]
                    if not op.is_dma:
                        rds[:] = [x for x in rds if x.eng != op.eng or x.is_dma]
                    rds.append(op)
                    return
            recs.append([p0, p1, f0, f1, None, [op]])

    def add(self, eng, fn, reads=(), writes=(), is_dma=False):
        op = Op(eng, len(self.ops[eng]), fn, is_dma)
        if is_dma:
            n = self.n_dma_q.get(eng, 0)
            self.n_dma_q[eng] = n + 1
            self.n_dma += 1
            if eng == "pool":
                base, cnt = self.n_slots - 12, 12
            else:
                base, cnt = 0, self.n_slots - 12
            op.slot = base + n % cnt
            op.slotval = 16 * (n // cnt + 1)
            prev = self.slot_last[op.slot]
            if prev is not None:
                self._need(op, prev)
            self.slot_last[op.slot] = op
        for ap in reads:
            self._access(op, ap, False)
        for ap in writes:
            self._access(op, ap, True)
        self.ops[eng].append(op)
        return op

    def dma(self, out, in_, eng="sp", is_output=False):
        op = self.add(eng, lambda e: e.dma_start(out=out, in_=in_), [in_], [out], is_dma=True)
        if is_output:
            self.all_dma_out.append(op)
        return op

    def mm(self, out, lhsT, rhs, start=True, stop=True, skip=False):
        return self.add("pe", lambda e: e.matmul(out, lhsT, rhs, start=start, stop=stop,
                                                 skip_group_check=skip), [lhsT, rhs], [out])

    def tr(self, out, in_, ident):
        return self.add("pe", lambda e: e.transpose(out, in_, ident), [in_, ident], [out])

    def act(self, out, in_, func, bias=0.0, scale=1.0, accum=None, eng="act"):
        rd = [in_]
        if not isinstance(bias, (int, float)):
            rd.append(bias)
        if not isinstance(scale, (int, float)):
            rd.append(scale)
        wr = [out]
        if accum is not None:
            wr.append(accum)

        def fn(e):
            if accum is not None:
                return e.activation(out, in_, func, bias=bias, scale=scale, accum_out=accum)
            return e.activation(out, in_, func, bias=bias, scale=scale)
        return self.add("act", fn, rd, wr)

    def tt(self, eng, out, in0, in1, op):
        return self.add(eng, lambda e: e.tensor_tensor(out, in0, in1, op), [in0, in1], [out])

    def ts(self, eng, out, in0, s1, s2, op0, op1=None, accum=None):
        rd = [in0]
        if not isinstance(s1, (int, float)):
            rd.append(s1)
        if s2 is not None and not isinstance(s2, (int, float)):
            rd.append(s2)
        wr = [out]
        if accum is not None:
            wr.append(accum)

        def fn(e):
            kw = {}
            if accum is not None:
                kw["accum_out"] = accum
            if op1 is None:
                return e.tensor_scalar(out, in0, s1, None, op0, **kw)
            return e.tensor_scalar(out, in0, s1, s2, op0, op1, **kw)
        return self.add(eng, fn, rd, wr)

    def stt(self, out, in0, scalar, in1, op0, op1, eng="dve"):
        rd = [in0, in1]
        if not isinstance(scalar, (int, float)):
            rd.append(scalar)
        return self.add(eng, lambda e: e.scalar_tensor_tensor(out, in0, scalar, in1, op0, op1), rd, [out])

    def copy(self, eng, out, in_):
        if eng == "act":
            return self.add("act", lambda e: e.copy(out, in_), [in_], [out])
        return self.add(eng, lambda e: e.tensor_copy(out, in_), [in_], [out])

    def memset(self, eng, ap, val):
        return self.add(eng, lambda e: e.memset(ap, val), [], [ap])

    def rsqrt(self, out, in_):
        self.act(out, in_, AF.Ln)
        return self.act(out, out, AF.Exp, scale=-0.5)

    def recip(self, out, in_):
        return self.add("dve", lambda e: e.reciprocal(out, in_), [in_], [out])

    def scan(self, out, d0, d1, init, op0, op1):
        rd = [d0, d1]
        if not isinstance(init, (int, float)):
            rd.append(init)
        return self.add("dve", lambda e: e.tensor_tensor_scan(out, d0, d1, init, op0, op1), rd, [out])

    def reduce(self, out, in_, op, axis=AX.X, eng="dve"):
        return self.add(eng, lambda e: e.tensor_reduce(out, in_, axis, op), [in_], [out])

    def emit(self, nc, block, sems, dsems):
        for e in ENGS:
            c = 0
            for op in self.ops[e]:
                if op.signal and not op.is_dma:
                    c += 1
                    op.semval = c
        engobj = {"pe": "tensor", "act": "scalar", "dve": "vector", "pool": "gpsimd", "sp": "sync"}
        outs = self.all_dma_out

        def make(ename):
            ops = self.ops[ename]

            def body(eng):
                for op in ops:
                    for key, dep in op.waits:
                        if dep.is_dma:
                            eng.wait_ge(dsems[dep.slot], dep.slotval)
                        else:
                            eng.wait_ge(sems[dep.eng], dep.semval)
                    ins = op.fn(eng)
                    if op.is_dma:
                        ins.then_inc(dsems[op.slot], 16)
                    elif op.signal:
                        ins.then_inc(sems[op.eng], 1)
                if ename == "sp":
                    for op in outs:
                        eng.wait_ge(dsems[op.slot], op.slotval)
            return body

        for e in ENGS:
            if not self.ops[e] and e != "sp":
                continue
            getattr(block, engobj[e])(make(e))


DEC_SCALE = float(np.exp(-0.5))
SLOPES = [2.0 ** (-8.0 * (h + 1) / 4.0) for h in range(4)]
LAMBDA_INIT = 0.8 - 0.6 * 1.0

PV_ANW, PV_FNW, PV_MU, PV_W0, PV_A0, PV_KK, PV_KA, PV_RK, PV_LNW, PV_LNB, PV_QW, PV_KW = \
    0, 8, 16, 30, 34, 38, 42, 46, 50, 54, 58, 59
PV_CW0, PV_CW1, PV_CW2, PV_CB = 60, 104, 148, 192
NPV = 236
C_BLK1, C_BLKM, C_NH, C_D = 0, 128, 256, 768
NCST = 768 + 2560
CB_ID, CB_SU, CB_SL, CB_UI, CB_I4, CB_RM = 0, 128, 640, 1152, 1664, 2176
CB_KAUG, CB_QAUG = 2688, 3200
NCB = 3712

ARENA_F32 = 52900


class Arena:
    def __init__(self, ar, n):
        self.ar = ar
        self.n = n
        self.p = 0
        self.hi = 0

    def f32(self, n):
        a = self.p
        self.p += n
        assert self.p <= self.n, ("arena overflow", self.p, self.n)
        self.hi = max(self.hi, self.p)
        return self.ar[:, a:a + n]

    def bf(self, n):
        assert n % 2 == 0
        a = self.p
        self.p += n // 2
        assert self.p <= self.n, ("arena overflow", self.p, self.n)
        self.hi = max(self.hi, self.p)
        return self.ar[:, a:a + n // 2].bitcast(BF16)


def build(T, do_a1=True, do_a2=True, do_b=True):
    from contextlib import ExitStack
    NST = T // 512
    NKT = T // 128
    nc = bass.Bass("TRN2", target_bir_lowering=False)

    def din(name, shape):
        return nc.dram_tensor(name, shape, F32, kind="ExternalInput").ap()

    x = din("x", [T, 1024])
    w_in = din("w_in", [1024, 3328])
    w_out = din("w_out", [1024, 1024])
    w_up = din("w_up", [1024, 5632])
    w_dn = din("w_dn", [2816, 1024])
    w_lora = din("w_lora", [128, 512])
    w_gate = din("w_gate", [128, 512])
    pvec = din("pvec", [128, NPV])
    cst = din("cst", [128, NCST])
    cstb = din("cstb", [128, NCB])
    lamv = din("lamv", [128, 256])
    subw = din("subw", [128, 128])
    out = nc.dram_tensor("out", [T, 1024], F32, kind="ExternalOutput").ap()

    S = Sched()
    es = ExitStack()
    with es:
        AR = es.enter_context(nc.sbuf_tensor("AR", [128, ARENA_F32], F32))
        al = Arena(AR, ARENA_F32)
        pbs = [es.enter_context(nc.psum_tensor("pb%d" % i, [128, 512], F32)) for i in range(6)]
        tqs = [es.enter_context(nc.psum_tensor("tq%d" % i, [128, 1024], BF16)) for i in range(2)]
        state = {"pb": 0, "tq": 0}

        def bank():
            b = pbs[state["pb"] % 6]
            state["pb"] += 1
            return b

        def tqhalf():
            k = state["tq"] % 4
            state["tq"] += 1
            return tqs[k // 2][:, (k % 2) * 512:(k % 2) * 512 + 512]

        pv = al.f32(NPV)
        der = al.f32(32)
        omu = der[:, 0:14]
        omka = der[:, 14:18]
        qws = der[:, 18:19]
        kws = der[:, 19:20]
        lam = der[:, 20:21]
        nlam = der[:, 21:22]
        cs = al.f32(768)
        blk1 = cs[:, C_BLK1:C_BLK1 + 128]
        blkm = cs[:, C_BLKM:C_BLKM + 128]
        negh = cs[:, C_NH:C_NH + 512]
        cb = al.bf(NCB)
        ident = cb[:, CB_ID:CB_ID + 128]
        SU4 = cb[:, CB_SU:CB_SU + 512]
        SL4 = cb[:, CB_SL:CB_SL + 512]
        UI4 = cb[:, CB_UI:CB_UI + 512]
        I4 = cb[:, CB_I4:CB_I4 + 512]
        rmask = cb[:, CB_RM:CB_RM + 512]
        kaug = cb[:, CB_KAUG:CB_KAUG + 512].rearrange("p (h k) -> p h k", h=4)
        qaug = cb[:, CB_QAUG:CB_QAUG + 512]
        hT = al.bf(4096).rearrange("p (k t) -> p k t", k=8)
        nrm = al.f32(16)
        NT = {}

        S.dma(pv, pvec)
        S.dma(cs, cst[:, 0:768])
        S.dma(cb, cstb, eng="pool")
        S.ts("dve", omu, pv[:, PV_MU:PV_MU + 14], -1.0, 1.0, ALU.mult, ALU.add)
        S.ts("dve", omka, pv[:, PV_KA:PV_KA + 4], -1.0, 1.0, ALU.mult, ALU.add)
        S.ts("dve", qws, pv[:, PV_QW:PV_QW + 1], 0.125, None, ALU.mult)
        S.ts("dve", kws, pv[:, PV_KW:PV_KW + 1], 1.0, None, ALU.mult)
        base_mark = al.p

        def norm_T_gen(xtile_of_j, wcol0, hT_=None, tqb=None, hn_=None, slack=1, defer=0, hns=None,
                       pool_rsqrt=False):
            hT_ = hT if hT_ is None else hT_
            hns = hns or [NT["hn"] if hn_ is None else hn_]
            junk = NT["junk"]
            tqb = tqb or [tqs[0], tqs[1]]
            wbc = pv[:, wcol0:wcol0 + 8].unsqueeze(2).to_broadcast([128, 8, 128])
            tiles = {0: xtile_of_j(0), 1: xtile_of_j(1)}
            pending = []
            clock = [0]

            def tick():
                clock[0] += 1
                while pending and pending[0][0] <= clock[0]:
                    pending.pop(0)[1]()

            def make_tr(j, hnj):
                def f():
                    tq = tqb[j % len(tqb)]
                    for kc in range(8):
                        S.tr(tq[:, kc * 128:(kc + 1) * 128], hnj[:, kc * 128:(kc + 1) * 128], ident)
                    S.tt("dve", hT_[:, :, j * 128:(j + 1) * 128],
                         tq[:, :].rearrange("p (k t) -> p k t", k=8), wbc, ALU.mult)
                return f
            yield
            yield
            for j in range(4):
                xt = tiles[j]
                hnj = hns[j % len(hns)]
                tick()
                yield
                S.act(junk, xt, AF.Square, accum=nrm[:, j:j + 1])
                tick()
                yield
                S.ts("dve", nrm[:, 4 + j:5 + j], nrm[:, j:j + 1], 1.0 / 1024.0, 1e-6, ALU.mult, ALU.add)
                tick()
                yield
                if pool_rsqrt:
                    S.tt("pool", nrm[:, 8 + j:9 + j], nrm[:, 4 + j:5 + j], negh[:, 0:1], ALU.pow)
                    tick()
                    yield
                else:
                    S.act(nrm[:, 8 + j:9 + j], nrm[:, 4 + j:5 + j], AF.Ln)
                    tick()
                    yield
                    S.act(nrm[:, 8 + j:9 + j], nrm[:, 8 + j:9 + j], AF.Exp, scale=-0.5)
                    tick()
                    yield
                S.ts("dve", hnj, xt, nrm[:, 8 + j:9 + j], None, ALU.mult)
                if j + 2 < 4:
                    tiles[j + 2] = xtile_of_j(j + 2)
                if defer > 0:
                    pending.append((clock[0] + defer, make_tr(j, hnj)))
                    tick()
                    yield
                else:
                    for _ in range(slack):
                        yield
                    make_tr(j, hnj)()
                    yield
            while pending:
                tick()
                yield

        def norm_T(xtile_of_j, wcol0):
            for _ in norm_T_gen(xtile_of_j, wcol0):
                pass

        def proj_fm(wt, col0):
            b = bank()
            for kc in range(8):
                S.mm(b[:], wt[:, kc, col0:col0 + 128], hT[:, kc, :], start=(kc == 0), stop=(kc == 7))
            return b

        pre_a2 = {}
        if do_a1 and do_a2:
            al.p = base_mark
            pre_a2["w2"] = al.bf(8 * 1536).rearrange("p (k n) -> p k n", k=8)
            pre_a2["wob"] = al.bf(4 * 1024).rearrange("p (k n) -> p k n", k=4)
            pre_a2["end"] = al.p
        if do_a1:
            al.p = base_mark
            NT["hn"] = al.bf(1024)
            NT["junk"] = al.bf(1024)
            w1 = al.bf(8 * 1792).rearrange("p (k n) -> p k n", k=8)
            w1_end = al.p
            woa = al.bf(4 * 1024).rearrange("p (k n) -> p k n", k=4)
            wlo = al.bf(512)
            wga = al.bf(512)
            xinA = [al.f32(1024) for _ in range(2)]
            hnA2 = al.bf(1024)
            zraw = [al.f32(514) for _ in range(2)]
            carry2 = [al.f32(16) for _ in range(2)]
            lora12 = al.bf(512)
            lora13 = al.bf(512)
            rT = al.bf(2048).rearrange("p (c t) -> p c t", c=4)
            aT = al.bf(2048).rearrange("p (c t) -> p c t", c=4)
            bT = al.bf(2048).rearrange("p (c t) -> p c t", c=4)
            kT = al.bf(2048).rearrange("p (c t) -> p c t", c=4)
            vT = al.bf(2048).rearrange("p (c t) -> p c t", c=4)
            tokV = al.bf(2048).rearrange("p (j f) -> p j f", j=4)
            tokB = al.bf(2048).rearrange("p (j f) -> p j f", j=4)
            tokK = al.bf(2048).rearrange("p (j f) -> p j f", j=4)
            gsb = al.bf(2048).rearrange("p (c t) -> p c t", c=4)
            bon = al.bf(2048).rearrange("p (c t) -> p c t", c=4)
            ysb = al.f32(2048).rearrange("p (c t) -> p c t", c=4)
            mixA = rT
            CH = [{"M": [al.bf(512) for _ in range(2)], "MT": [al.bf(512) for _ in range(2)],
                   "P": [al.bf(512) for _ in range(2)]} for _ in range(4)]
            AM = [[al.bf(1024) for _ in range(3)] for _ in range(2)]
            Gbf = al.bf(512)
            Ubf = al.bf(512)
            S0T = al.f32(256)
            S0bf = al.bf(256)
            tmpS = al.f32(256)
            Wc = al.f32(16).rearrange("p (c j) -> p c j", c=4)
            tl = [al.f32(512) for _ in range(9)]
            ysb_flat = ysb.rearrange("p c t -> p (c t)")
            tlb = [ysb_flat[:, i * 512:(i + 1) * 512] for i in range(4)] + [al.f32(512) for _ in range(5)]
            TS = [{"tl": tl, "zraw": zraw, "t1": tl[8]},
                  {"tl": tlb, "zraw": [al.f32(514) for _ in range(2)], "t1": tlb[8]}]
            zl = al.f32(512)
            eps12 = al.f32(2)
            S.memset("pool", eps12, 1e-12)
            epsln = al.f32(2)
            S.memset("pool", epsln, 64e-5)
            print("A1 arena words", al.p)

            for kc in range(8):
                S.dma(w1[:, kc, :], w_in[kc * 128:(kc + 1) * 128, 0:1792], eng="pool")
            for kc in range(4):
                S.dma(woa[:, kc, :], w_out[kc * 128:(kc + 1) * 128, :], eng="pool")
            S.dma(wlo, w_lora, eng="pool")
            S.dma(wga, w_gate, eng="pool")
            for c_ in carry2:
                S.memset("pool", c_, 0.0)
            S.memset("pool", S0T, 0.0)
            S.memset("pool", S0bf, 0.0)

            cur_s = [0]

            def lerp_chunk(c, b, zout, ts_=None):
                s_ = cur_s[0]
                cold, cnew = carry2[(s_ + 1) % 2], carry2[s_ % 2]
                mu_ = pv[:, PV_MU + c:PV_MU + c + 1]
                S.act(zout, b[:], AF.Copy, scale=omu[:, c:c + 1])
                S.copy("act", cnew[:, c:c + 1], b[:, 511:512])
                S.stt(zout[:, 1:512], b[:, 0:511], mu_, zout[:, 1:512], ALU.mult, ALU.add)
                S.stt(zout[:, 0:1], cold[:, c:c + 1], mu_, zout[:, 0:1], ALU.mult, ALU.add)

            def gen_normA(s_):
                t0_ = s_ * 512

                def xload(j):
                    xt = xinA[j % 2]
                    S.dma(xt, x[t0_ + j * 128:t0_ + (j + 1) * 128, :])
                    return xt
                yield from norm_T_gen(xload, PV_ANW, defer=6, hns=[NT["hn"], hnA2])

            for _ in gen_normA(0):
                pass
            for s in range(NST):
                t0 = s * 512
                cur_s[0] = s
                bgn = gen_normA(s + 1) if s + 1 < NST else iter(())
                def gen_lora():
                    lerp_chunk(12, proj_fm(w1, 12 * 128), zl)
                    yield
                    S.act(lora12[0:64, :], zl[0:64, :], AF.Tanh)
                    S.copy("pool", lora12[64:128, :], zl[64:128, :])
                    yield
                    lerp_chunk(13, proj_fm(w1, 13 * 128), zl)
                    yield
                    S.act(lora13, zl, AF.Sigmoid)
                    yield
                def gen_pair(c, ts_, delay):
                    for _ in range(delay):
                        yield
                    tl_ = ts_["tl"]
                    zr_, zk_, zv_ = tl_[0], tl_[1], tl_[2]
                    lerp_chunk(c, proj_fm(w1, c * 128), zr_, ts_)
                    yield
                    lerp_chunk(4 + c, proj_fm(w1, (4 + c) * 128), zk_, ts_)
                    yield
                    lerp_chunk(8 + c, proj_fm(w1, (8 + c) * 128), zv_, ts_)
                    yield
                    cc = slice(c * 128, (c + 1) * 128)
                    b = bank()
                    S.mm(b[:], wlo[0:64, cc], lora12[0:64, :])
                    sg = tl_[3]
                    S.act(sg, b[:], AF.Sigmoid, bias=pv[:, PV_W0 + c:PV_W0 + c + 1])
                    b = bank()
                    S.mm(b[:], wlo[64:128, cc], lora12[64:128, :])
                    apm = tl_[8]
                    S.act(apm, b[:], AF.Sigmoid, bias=pv[:, PV_A0 + c:PV_A0 + c + 1])
                    yield
                    Lc = tl_[4]
                    S.scan(Lc, rmask, sg, 0.0, ALU.mult, ALU.add)
                    b = bank()
                    S.mm(b[:], wga[:, cc], lora13)
                    S.copy("act", gsb[:, c, :], b[:])
                    kk = tl_[6]
                    S.ts("dve", kk, zk_, pv[:, PV_KK + c:PV_KK + c + 1], None, ALU.mult)
                    kk2 = tl_[7]
                    S.act(kk2, zk_, AF.Square, scale=pv[:, PV_KK + c:PV_KK + c + 1])
                    b = bank()
                    S.mm(b[:], blk1, kk2)
                    yield
                    eL, enL = tl_[3], tl_[5]
                    S.act(eL, Lc, AF.Exp, scale=-DEC_SCALE)
                    S.act(enL, Lc, AF.Exp, scale=DEC_SCALE)
                    S.copy("pool", Wc[:, c, :], eL[:, 127:512:128])
                    S.act(kk2, b[:], AF.Ln, bias=eps12[:, 0:1])
                    S.act(kk2, kk2, AF.Exp, scale=-0.5)
                    yield
                    t4 = tl_[4]
                    S.ts("dve", t4, apm, pv[:, PV_KA + c:PV_KA + c + 1], omka[:, c:c + 1], ALU.mult, ALU.add)
                    S.tt("pool", t4, zk_, t4, ALU.mult)
                    S.tt("pool", apm, apm, enL, ALU.mult)
                    yield
                    S.tt("dve", kT[:, c, :], t4, enL, ALU.mult)
                    S.tt("dve", rT[:, c, :], zr_, eL, ALU.mult)
                    rk = tl_[1]
                    S.stt(rk, zr_, pv[:, PV_RK + c:PV_RK + c + 1], t4, ALU.mult, ALU.mult)
                    bb_ = bank()
                    S.mm(bb_[:], blk1, rk)
                    S.copy("act", vT[:, c, :], zv_)
                    yield
                    S.tt("dve", kk, kk, kk2, ALU.mult)
                    S.stt(aT[:, c, 1:512], kk[:, 1:512], -1.0, eL[:, 0:511], ALU.mult, ALU.mult)
                    S.ts("dve", aT[:, c, 0:512:128], kk[:, 0:512:128], -1.0, None, ALU.mult)
                    S.tt("dve", bT[:, c, :], kk, apm, ALU.mult)
                    yield
                    S.tt("dve", bon[:, c, :], bb_[:], zv_, ALU.mult)
                    yield

                def seq(*gs):
                    for g in gs:
                        yield from g

                def rr(gens):
                    gens = list(gens)
                    while gens:
                        for g in list(gens):
                            try:
                                next(g)
                            except StopIteration:
                                gens.remove(g)

                if True:
                    rr([seq(gen_pair(0, TS[0], 0), gen_pair(2, TS[0], 0)),
                        seq(gen_pair(1, TS[1], 1), gen_pair(3, TS[1], 0)), gen_lora()])
                if s == NST - 1 and pre_a2:
                    assert pre_a2["end"] <= w1_end, (pre_a2["end"], w1_end)
                    for kc in range(8):
                        S.dma(pre_a2["w2"][:, kc, :], w_in[kc * 128:(kc + 1) * 128, 1792:3328], eng="pool")
                    for kc in range(4):
                        S.dma(pre_a2["wob"][:, kc, :], w_out[512 + kc * 128:512 + (kc + 1) * 128, :], eng="pool")
                    pre_a2["done"] = True
                def gen_tok():
                    for j in range(4):
                        for (src_, dst) in ((vT, tokV), (bT, tokB), (kT, tokK)):
                            tq = tqhalf()
                            for c in range(4):
                                S.tr(tq[:, c * 128:(c + 1) * 128], src_[:, c, j * 128:(j + 1) * 128], ident)
                            S.copy("act" if dst is tokB else "dve", dst[:, j, :], tq)
                            yield
                def gen_chain(j, gi):
                    tj = slice(j * 128, (j + 1) * 128)
                    ch = CH[(j % 2) * 2 + gi]
                    Mb_, MTb_, Pb_ = ch["M"], ch["MT"], ch["P"]
                    AkT_, RbT_, RkT_ = AM[j % 2]
                    r0 = gi * 64
                    rows = slice(r0, r0 + 64)
                    gsl = slice(gi * 512, (gi + 1) * 512)
                    for typ in range(5):
                        b = bank()
                        for hh in range(4):
                            A_, B_, K_, R_ = aT[rows, hh, tj], bT[rows, hh, tj], kT[rows, hh, tj], rT[rows, hh, tj]
                            lhsT, rhs = ((B_, A_), (A_, B_), (K_, A_), (B_, R_), (K_, R_))[typ]
                            S.mm(b[:, hh * 128:(hh + 1) * 128], lhsT, rhs)
                        if typ == 0:
                            S.tt("dve", Mb_[0], b[:], SU4, ALU.mult)
                            S.tt("pool", Pb_[0], Mb_[0], I4, ALU.add)
                        elif typ == 1:
                            S.tt("dve", MTb_[0], b[:], SL4, ALU.mult)
                        elif typ == 2:
                            S.tt("dve", AkT_[:, gsl], b[:], SU4, ALU.mult)
                        elif typ == 3:
                            S.tt("dve", RbT_[:, gsl], b[:], UI4, ALU.mult)
                        else:
                            S.tt("dve", RkT_[:, gsl], b[:], UI4, ALU.mult)
                        yield
                    for l in range(1, 7):
                        sr, ds = (l - 1) % 2, l % 2
                        if l < 6:
                            b = bank()
                            for hh in range(4):
                                q = slice(hh * 128, (hh + 1) * 128)
                                S.mm(b[:, q], MTb_[sr][:, q], Mb_[sr][:, q])
                            S.copy("act", Mb_[ds], b[:])
                        b = bank()
                        for hh in range(4):
                            q = slice(hh * 128, (hh + 1) * 128)
                            S.mm(b[:, q], Mb_[sr][:, q], MTb_[sr][:, q])
                        S.copy("act", MTb_[ds], b[:])
                        yield
                        b = bank()
                        for hh in range(4):
                            q = slice(hh * 128, (hh + 1) * 128)
                            S.mm(b[:, q], MTb_[ds][:, q], Pb_[sr][:, q])
                        S.tt("dve", Pb_[ds], b[:], Pb_[sr], ALU.add)
                        yield

                def gen_rec(j):
                    tj = slice(j * 128, (j + 1) * 128)
                    AkT_, RbT_, RkT_ = AM[j % 2]
                    TT = [CH[(j % 2) * 2 + gi]["P"][0] for gi in range(2)]
                    for gi in range(2):
                        r0 = gi * 64
                        rows = slice(r0, r0 + 64)
                        bG = bank()
                        for hh in range(4):
                            fs = slice(hh * 128 + r0, hh * 128 + r0 + 64)
                            o = bG[:, hh * 64:(hh + 1) * 64]
                            S.mm(o, aT[rows, hh, tj], S0bf[rows, hh * 64:(hh + 1) * 64], start=True, stop=False)
                            S.mm(o, AkT_[:, gi * 512 + hh * 128:gi * 512 + (hh + 1) * 128], tokV[:, j, fs],
                                 start=False, stop=True)
                        S.copy("act" if gi == 0 else "dve", Gbf[:, gi * 256:(gi + 1) * 256], bG[:, 0:256])
                    yield
                    bU = bank()
                    for gi in range(2):
                        for hh in range(4):
                            sl_ = slice(gi * 256 + hh * 64, gi * 256 + (hh + 1) * 64)
                            S.mm(bU[:, sl_], TT[gi][:, hh * 128:(hh + 1) * 128], Gbf[:, sl_])
                    S.copy("act", Ubf, bU[:])
                    yield
                    bS = bank()
                    for gi in range(2):
                        r0 = gi * 64
                        rows = slice(r0, r0 + 64)
                        for hh in range(4):
                            fs = slice(hh * 128 + r0, hh * 128 + r0 + 64)
                            sl_ = slice(gi * 256 + hh * 64, gi * 256 + (hh + 1) * 64)
                            o = bS[rows, hh * 64:(hh + 1) * 64]
                            S.mm(o, tokB[:, j, fs], Ubf[:, sl_], start=True, stop=False)
                            S.mm(o, tokK[:, j, fs], tokV[:, j, fs], start=False, stop=True)
                    for gi in range(2):
                        r0 = gi * 64
                        rows = slice(r0, r0 + 64)
                        bY = bank()
                        for hh in range(4):
                            fs = slice(hh * 128 + r0, hh * 128 + r0 + 64)
                            sl_ = slice(gi * 256 + hh * 64, gi * 256 + (hh + 1) * 64)
                            o = bY[rows, hh * 128:(hh + 1) * 128]
                            S.mm(o, S0bf[rows, hh * 64:(hh + 1) * 64], rT[rows, hh, tj], start=True, stop=False)
                            S.mm(o, Ubf[:, sl_], RbT_[:, gi * 512 + hh * 128:gi * 512 + (hh + 1) * 128],
                                 start=False, stop=False)
                            S.mm(o, tokV[:, j, fs], RkT_[:, gi * 512 + hh * 128:gi * 512 + (hh + 1) * 128],
                                 start=False, stop=True)
                        S.copy("act", ysb[rows, :, tj], bY[rows, :].rearrange("p (c t) -> p c t", c=4))
                    S.tt("dve", tmpS, bS[:, 0:256], S0T, ALU.add)
                    S.tt("dve", S0T.rearrange("p (c v) -> p c v", c=4),
                         tmpS.rearrange("p (c v) -> p c v", c=4),
                         Wc[:, :, j:j + 1].to_broadcast([128, 4, 64]), ALU.mult)
                    S.copy("pool", S0bf, S0T)
                    yield

                def seq(*gs):
                    for g in gs:
                        yield from g

                def rr(gens, bg=None):
                    gens = list(gens)
                    while gens:
                        for g in list(gens):
                            try:
                                next(g)
                            except StopIteration:
                                gens.remove(g)
                        if bg is not None:
                            next(bg, None)

                rr([gen_chain(0, 0), gen_chain(0, 1), gen_chain(1, 0), gen_chain(1, 1), gen_tok()], bg=bgn)
                rr([gen_rec(0)], bg=bgn)
                rr([gen_rec(1), gen_chain(2, 0), gen_chain(2, 1)], bg=bgn)
                rr([gen_rec(2), gen_chain(3, 0), gen_chain(3, 1)], bg=bgn)
                rr([gen_rec(3)], bg=bgn)
                for _ in bgn:
                    pass
                def gen_gn(c, d, dsq, delay):
                    for _ in range(delay):
                        yield
                    b1 = bank()
                    S.mm(b1[:], blkm, ysb[:, c, :])
                    yield
                    S.tt("dve", d, ysb[:, c, :], b1[:], ALU.subtract)
                    yield
                    S.tt("pool", dsq, d, d, ALU.mult)
                    yield
                    b2 = bank()
                    S.mm(b2[:], blkm, dsq)
                    yield
                    S.act(dsq, b2[:], AF.Ln, bias=epsln[:, 0:1])
                    yield
                    S.act(dsq, dsq, AF.Exp, scale=-0.5)
                    yield
                    S.tt("dve", d, d, dsq, ALU.mult)
                    S.ts("dve", d, d, pv[:, PV_LNW + c:PV_LNW + c + 1], pv[:, PV_LNB + c:PV_LNB + c + 1],
                         ALU.mult, ALU.add)
                    yield
                    S.tt("pool", d, d, bon[:, c, :], ALU.add)
                    yield
                    S.tt("dve", mixA[:, c, :], d, gsb[:, c, :], ALU.mult)
                    yield

                if True:
                    rr([seq(gen_gn(0, tl[0], tl[1], 0), gen_gn(2, tl[0], tl[1], 0)),
                        seq(gen_gn(1, tlb[4], tlb[5], 1), gen_gn(3, tlb[4], tlb[5], 0))])
                xt_ = {}
                for j in range(2):
                    xt_[j] = xinA[j % 2]
                    S.dma(xt_[j], x[t0 + j * 128:t0 + (j + 1) * 128, :])
                for j in range(4):
                    tj = slice(j * 128, (j + 1) * 128)
                    xt = xt_[j]
                    for half in range(2):
                        b = bank()
                        hsl = slice(half * 512, (half + 1) * 512)
                        for c in range(4):
                            S.mm(b[:], mixA[:, c, tj], woa[:, c, hsl], start=(c == 0), stop=(c == 3))
                        S.tt("dve", xt[:, hsl], b[:], xt[:, hsl], ALU.add)
                    S.dma(out[t0 + j * 128:t0 + (j + 1) * 128, :], xt, is_output=True)
                    if j + 2 < 4:
                        xt_[j + 2] = xinA[j % 2]
                        S.dma(xt_[j + 2], x[t0 + (j + 2) * 128:t0 + (j + 3) * 128, :])
        src_a2 = out if do_a1 else x

        if do_a2:
            al.p = base_mark
            w2 = al.bf(8 * 1536).rearrange("p (k n) -> p k n", k=8)
            wob = al.bf(4 * 1024).rearrange("p (k n) -> p k n", k=4)
            KA = [al.bf(4 * T).rearrange("p (h t) -> p h t", h=4) for _ in range(2)]
            Vat = al.bf(NKT * 4 * 130).rearrange("p (k h d) -> p k h d", k=NKT, h=4)
            Dt = al.f32(128)
            rs2 = [al.f32(512) for _ in range(2)]
            hn3 = al.bf(1024)
            subr = al.f32(128)
            lmv = al.f32(256)
            xin = [al.f32(1024) for _ in range(2)]
            QA = [[al.bf(2048).rearrange("p (h t) -> p h t", h=4) for _ in range(2)] for _ in range(2)]
            mixB = al.bf(2048).rearrange("p (h t) -> p h t", h=4)
            pT = [al.bf(512) for _ in range(4)]
            tn = [al.f32(512).rearrange("p (q d) -> p q d", q=4) for _ in range(2)]
            etmp = [al.f32(512) for _ in range(2)]
            tl2 = [al.f32(512) for _ in range(2)]
            Ot = al.f32(512).rearrange("p (q d) -> p q d", q=4)
            Obf = al.bf(512).rearrange("p (q d) -> p q d", q=4)
            sm = al.f32(32)
            hn2 = al.bf(1024)
            NT["hn"] = hn2
            NT["junk"] = tl2[1].bitcast(BF16)
            eps6 = al.f32(2)
            S.memset("pool", eps6, 1e-6)
            print("A2 arena words", al.p)

            if not pre_a2.get("done"):
                for kc in range(8):
                    S.dma(w2[:, kc, :], w_in[kc * 128:(kc + 1) * 128, 1792:3328], eng="pool")
                for kc in range(4):
                    S.dma(wob[:, kc, :], w_out[512 + kc * 128:512 + (kc + 1) * 128, :], eng="pool")
            S.dma(Dt, cst[:, C_D:C_D + 128])
            for m in range(2):
                ar0 = 64 * (1 - m)
                S.memset("dve", KA[m].rearrange("p h t -> p (h t)"), 0.0)
                for qq in range(2):
                    S.memset("dve", QA[qq][m].rearrange("p h t -> p (h t)"), 0.0)
                for kt in range(NKT):
                    S.copy("pool", KA[m][ar0:ar0 + 3, :, kt * 128:(kt + 1) * 128], kaug[ar0:ar0 + 3, :, :])
                for qq in range(2):
                    for h in range(4):
                        S.copy("pool", QA[qq][m][ar0:ar0 + 3, h, :], qaug[ar0:ar0 + 3, :])
            S.dma(subr, subw)
            S.dma(lmv, lamv)
            S.ts("dve", subr, subr, 1.0 - LAMBDA_INIT, None, ALU.mult)
            S.tt("dve", tl2[0][:, 0:64], lmv[:, 0:64], lmv[:, 64:128], ALU.mult)
            S.tt("dve", tl2[0][:, 64:128], lmv[:, 128:192], lmv[:, 192:256], ALU.mult)
            S.reduce(sm[:, 0:2], tl2[0][:, 0:128].rearrange("p (a d) -> p a d", a=2), ALU.add)
            S.act(sm[:, 2:4], sm[:, 0:2], AF.Exp)
            S.tt("dve", sm[:, 4:5], sm[:, 2:3], sm[:, 3:4], ALU.subtract)
            S.ts("dve", nlam, sm[:, 4:5], -1.0, -LAMBDA_INIT, ALU.mult, ALU.add)
            for kt in range(NKT):
                S.memset("pool", Vat[:, kt, :, 128:129], 1.0)

            gb_state = {"i": 0}
            gbanks = [pbs[5], tqs[0][:].bitcast(F32)]

            def gbank():
                gb_state["i"] += 1
                return gbanks[gb_state["i"] % 2]

            def gen_pre(s):
                t0 = s * 512
                qa = QA[s % 2]

                def xload(j):
                    xt = xin[j % 2]
                    S.dma(xt, x[t0 + j * 128:t0 + (j + 1) * 128, :])
                    return xt
                yield from norm_T_gen(xload, PV_ANW, tqb=[tqs[1]], hns=[hn2, hn3], defer=3)

                def gen_qk(chunks, raw, sq, delay):
                    for _ in range(delay):
                        yield
                    for (h, which) in chunks:
                        b = gbank()
                        col0 = which * 512 + h * 128
                        for kc in range(8):
                            S.mm(b[:, :], w2[:, kc, col0:col0 + 128], hT[:, kc, :], start=(kc == 0), stop=(kc == 7))
                        yield
                        S.copy("dve", raw, b[:, :])
                        S.act(sq, b[:, :], AF.Square)
                        yield
                        b2 = gbank()
                        S.mm(b2[:, :], blkm, sq)
                        yield
                        S.act(sq, b2[:, :], AF.Ln, bias=eps6[:, 0:1])
                        yield
                        S.act(sq, sq, AF.Exp, scale=-0.5)
                        yield
                        for m in range(2):
                            rr_ = slice(m * 64, (m + 1) * 64)
                            if which == 0:
                                S.stt(qa[m][rr_, h, :], raw[rr_, :], qws[rr_, :], sq[rr_, :], ALU.mult, ALU.mult)
                            else:
                                S.stt(KA[m][rr_, h, t0:t0 + 512], raw[rr_, :], kws[rr_, :], sq[rr_, :],
                                      ALU.mult, ALU.mult)
                        yield
                chunks = [(h, which) for h in range(4) for which in range(2)]
                ga = gen_qk(chunks[0::2], tl2[0], tl2[1], 0)
                gb = gen_qk(chunks[1::2], rs2[0], rs2[1], 1)
                live = [ga, gb]
                while live:
                    for g in list(live):
                        try:
                            next(g)
                        except StopIteration:
                            live.remove(g)
                    yield
                for j in range(4):
                    b = gbank()
                    for kc in range(8):
                        S.mm(b[:, :], hT[:, kc, j * 128:(j + 1) * 128], w2[:, kc, 1024:1536],
                             start=(kc == 0), stop=(kc == 7))
                    yield
                    S.copy("dve", Vat[:, 4 * s + j, :, 0:128], b[:, :].rearrange("p (h d) -> p h d", h=4))
                    yield

            def gen_post(s):
                t0 = s * 512
                for j in range(4):
                    tj = slice(j * 128, (j + 1) * 128)
                    xt = xin[j % 2]
                    S.dma(xt, src_a2[t0 + j * 128:t0 + (j + 1) * 128, :])
                    bs_ = []
                    for half in range(2):
                        b = gbank()
                        bs_.append(b)
                        hsl = slice(half * 512, (half + 1) * 512)
                        for c in range(4):
                            S.mm(b[:, :], mixB[:, c, tj], wob[:, c, hsl], start=(c == 0), stop=(c == 3))
                    for half in range(2):
                        hsl = slice(half * 512, (half + 1) * 512)
                        S.tt("dve", xt[:, hsl], bs_[half][:, :], xt[:, hsl], ALU.add)
                    S.dma(out[t0 + j * 128:t0 + (j + 1) * 128, :], xt, is_output=True)
                    yield

            def gen_loop(s):
                qa = QA[s % 2]
                nkt = 4 * s + 4
                steps = [(h, m, kt) for h in range(4) for m in range(2) for kt in range(nkt)]
                scb = [pbs[2], pbs[3], pbs[4]]
                accp = [pbs[0], pbs[1]]
                LA = 2
                started = {}

                def emit_score(i):
                    h, m, kt = steps[i]
                    jd = kt - 4 * s
                    q0 = 128 * jd if jd > 0 else 0
                    sc = scb[i % 3]
                    r0 = m * 64
                    rows = slice(r0, r0 + 64)
                    S.mm(sc[:, q0:512], KA[m][:, h, kt * 128:(kt + 1) * 128], qa[m][:, h, q0:512])

                def emit_rest(i):
                    h, m, kt = steps[i]
                    u = h * 2 + m
                    jd = kt - 4 * s
                    q0 = 128 * jd if jd > 0 else 0
                    sc = scb[i % 3]
                    pt = pT[i % 4]
                    cimm = -SLOPES[h] * (512 * s - 128 * kt - 128)
                    if jd >= 0:
                        blk = sc[:, q0:q0 + 128]
                        S.stt(blk, Dt[:, 0:128], float(SLOPES[h]), blk, ALU.mult, ALU.add)
                    S.act(pt[:, q0:512], sc[:, q0:512], AF.Exp, bias=float(cimm))
                    for qj in range(max(jd, 0), 4):
                        ab = accp[qj // 2]
                        o = ab[:, (qj % 2) * 130:(qj % 2) * 130 + 129]
                        key = (u, qj // 2)
                        st = key not in started
                        started[key] = True
                        S.mm(o, pt[:, qj * 128:(qj + 1) * 128], Vat[:, kt, h, 0:129],
                             start=st, stop=(kt == 4 * s + qj), skip=True)
                    if kt == nkt - 1:
                        for bb in range(2):
                            S.recip(sm[:, 8 + bb * 2:8 + bb * 2 + 2], accp[bb][:, 128:260:130])
                            S.tt("dve", tn[m][:, 2 * bb:2 * bb + 2, :],
                                 accp[bb][:, 0:260].rearrange("p (q d) -> p q d", q=2)[:, :, 0:128],
                                 sm[:, 8 + bb * 2:8 + bb * 2 + 2].unsqueeze(2).to_broadcast([128, 2, 128]), ALU.mult)
                        if m == 1:
                            Of = Ot.rearrange("p q d -> p (q d)")
                            S.stt(Of, tn[1].rearrange("p q d -> p (q d)"), nlam,
                                  tn[0].rearrange("p q d -> p (q d)"), ALU.mult, ALU.add)
                            S.tt("pool", tn[0].rearrange("p q d -> p (q d)"), Of, Of, ALU.mult)
                            S.reduce(sm[:, 16:20], tn[0], ALU.add)
                            S.ts("dve", sm[:, 20:24], sm[:, 16:20], 1.0 / 128.0, 1e-6, ALU.mult, ALU.add)

                            def stage_b():
                                S.rsqrt(sm[:, 24:28], sm[:, 20:24])

                            def stage_c(h=h):
                                S.tt("dve", Ot, Ot, sm[:, 24:28].unsqueeze(2).to_broadcast([128, 4, 128]), ALU.mult)
                                S.tt("pool", Obf, Ot, subr.unsqueeze(1).to_broadcast([128, 4, 128]), ALU.mult)
                                tq = tqs[1][:, 0:512]
                                for qj in range(4):
                                    S.tr(tq[:, qj * 128:(qj + 1) * 128], Obf[:, qj, :], ident)
                                S.copy("dve", mixB[:, h, :], tq)
                            deferred.append((i + LA + dB, stage_b))
                            deferred.append((i + LA + dC, stage_c))

                dB, dC = (3, 5) if nkt < 8 else (8, 12)
                deferred = []

                def run_deferred(now):
                    while deferred and deferred[0][0] <= now:
                        deferred.pop(0)[1]()

                for i in range(len(steps) + LA):
                    if i < len(steps):
                        emit_score(i)
                    if i >= LA:
                        emit_rest(i - LA)
                    run_deferred(i)
                    yield
                run_deferred(1 << 30)

            def seq2(*gs):
                for g in gs:
                    yield from g

            def rr2(gens):
                gens = [g if isinstance(g, tuple) else (g, 1) for g in gens]
                while gens:
                    for g in list(gens):
                        try:
                            for _ in range(g[1]):
                                next(g[0])
                        except StopIteration:
                            gens.remove(g)

            rr2([gen_pre(0)])
            for s in range(NST):
                others = []
                if s > 0:
                    others.append(gen_post(s - 1))
                if s + 1 < NST:
                    others.append(gen_pre(s + 1))
                nsteps = 8 * (4 * s + 4)
                rr2([(gen_loop(s), 3 if s < 6 else 4), (seq2(*others), 1)])
            rr2([gen_post(NST - 1)])
        src_b = out if (do_a1 or do_a2) else x

        if do_b:
            al.p = base_mark
            NT["hn"] = al.bf(1024)
            NT["junk"] = al.bf(1024)
            wup = al.bf(8 * 5632).rearrange("p (k n) -> p k n", k=8)
            wdn = al.bf(22 * 1024).rearrange("p (k n) -> p k n", k=22)
            gT = al.bf(22 * 512).rearrange("p (c t) -> p c t", c=22)
            xin = [al.f32(1024) for _ in range(2)]
            ccab = [al.f32(88).rearrange("p (c k) -> p c k", c=44) for _ in range(2)]
            tb = [al.f32(512) for _ in range(3)]
            hT2 = al.bf(4096).rearrange("p (k t) -> p k t", k=8)
            hnB2 = al.bf(1024)
            print("B arena words", al.p)
            for kc in range(8):
                for part in range(2):
                    S.dma(wup[:, kc, part * 2816:(part + 1) * 2816],
                          w_up[kc * 128:(kc + 1) * 128, part * 2816:(part + 1) * 2816], eng="pool")
            for kc in range(22):
                S.dma(wdn[:, kc, :], w_dn[kc * 128:(kc + 1) * 128, :], eng="pool")
            for cc_ in ccab:
                S.memset("pool", cc_.rearrange("p c k -> p (c k)"), 0.0)

            hTb = [hT, hT2]

            def gen_normB(s):
                t0 = s * 512

                def xload(j):
                    xt = xin[j % 2]
                    S.dma(xt, src_b[t0 + j * 128:t0 + (j + 1) * 128, :])
                    return xt
                yield from norm_T_gen(xload, PV_FNW, hT_=hTb[s % 2], defer=9, hns=[NT["hn"], hnB2],
                                      pool_rsqrt=True)

            def gen_ffn(s):
                hcur = hTb[s % 2]
                n = 0
                for cg in range(22):
                    res = []
                    for ch in (cg, 22 + cg):
                        b = bank()
                        for kc in range(8):
                            S.mm(b[:], wup[:, kc, ch * 128:(ch + 1) * 128], hcur[:, kc, :],
                                 start=(kc == 0), stop=(kc == 7))
                        t1 = tb[n % 2]
                        n += 1
                        cold, cnew = ccab[(s + 1) % 2], ccab[s % 2]
                        w0_ = pv[:, PV_CW0 + ch:PV_CW0 + ch + 1]
                        w1_ = pv[:, PV_CW1 + ch:PV_CW1 + ch + 1]
                        S.act(t1, b[:], AF.Identity, scale=pv[:, PV_CW2 + ch:PV_CW2 + ch + 1],
                              bias=pv[:, PV_CB + ch:PV_CB + ch + 1])
                        S.copy("act", cnew[:, ch, :], b[:, 510:512])
                        S.stt(t1[:, 1:512], b[:, 0:511], w1_, t1[:, 1:512], ALU.mult, ALU.add)
                        S.stt(t1[:, 0:1], cold[:, ch, 1:2], w1_, t1[:, 0:1], ALU.mult, ALU.add)
                        S.stt(t1[:, 2:512], b[:, 0:510], w0_, t1[:, 2:512], ALU.mult, ALU.add)
                        S.stt(t1[:, 0:2], cold[:, ch, 0:2], w0_, t1[:, 0:2], ALU.mult, ALU.add)
                        res.append(t1)
                        yield
                    sg = tb[2]
                    S.act(sg, res[0], AF.Silu)
                    S.tt("pool", gT[:, cg, :], sg, res[1], ALU.mult)

            def postB(s):
                t0 = s * 512
                tiles = {}
                for j in range(2):
                    tiles[j] = xin[j % 2]
                    S.dma(tiles[j], src_b[t0 + j * 128:t0 + (j + 1) * 128, :])
                for j in range(4):
                    tj = slice(j * 128, (j + 1) * 128)
                    xt = tiles[j]
                    for half in range(2):
                        b = bank()
                        hsl = slice(half * 512, (half + 1) * 512)
                        for cg in range(22):
                            S.mm(b[:], gT[:, cg, tj], wdn[:, cg, hsl], start=(cg == 0), stop=(cg == 21))
                        S.tt("dve", xt[:, hsl], b[:], xt[:, hsl], ALU.add)
                    S.dma(out[t0 + j * 128:t0 + (j + 1) * 128, :], xt, is_output=True)
                    if j + 2 < 4:
                        tiles[j + 2] = xin[j % 2]
                        S.dma(tiles[j + 2], src_b[t0 + (j + 2) * 128:t0 + (j + 3) * 128, :])

            def rrB(gens):
                gens = list(gens)
                while gens:
                    for g in list(gens):
                        try:
                            next(g)
                        except StopIteration:
                            gens.remove(g)

            rrB([gen_normB(0)])
            for s in range(NST):
                gs = [gen_ffn(s)]
                if s + 1 < NST:
                    gs.append(gen_normB(s + 1))
                rrB(gs)
                postB(s)

        print("arena hi words", al.hi, "of", ARENA_F32, "n ops", {e: len(S.ops[e]) for e in ENGS})
        sems = {e: es.enter_context(nc.semaphore("s_" + e)) for e in ENGS}
        dsems = [es.enter_context(nc.semaphore("d%d" % i)) for i in range(S.n_slots)]
        with nc.Block() as block:
            S.emit(nc, block, sems, dsems)
    return nc


def _cols(v, nchunk):
    return np.ascontiguousarray(np.asarray(v, np.float32).reshape(nchunk, 128).T)


def host_consts():
    p = np.arange(128)
    blk1 = (p[:, None] // 64 == p[None, :] // 64).astype(np.float32)
    cst = np.zeros((128, NCST), np.float32)
    cst[:, C_BLK1:C_BLK1 + 128] = blk1
    cst[:, C_BLKM:C_BLKM + 128] = blk1 / 64.0
    cst[:, C_NH:C_NH + 512] = -0.5
    xq = np.arange(640)
    allowed = (p[:, None] // 64) <= (xq[None, :] // 64)
    d = 2.0 * np.minimum(xq[None, :] - p[:, None], 0).astype(np.float32)
    cst[:, C_D:C_D + 640] = np.where(allowed, d, -1.0e6)
    cb = np.zeros((128, NCB), np.float32)
    col = np.arange(128)
    eye = np.eye(128, dtype=np.float32)
    su = (col[None, :] > p[:, None]).astype(np.float32)
    sl = (col[None, :] < p[:, None]).astype(np.float32)
    ui = (col[None, :] >= p[:, None]).astype(np.float32)
    cb[:, CB_ID:CB_ID + 128] = eye
    cb[:, CB_SU:CB_SU + 512] = np.tile(su, (1, 4))
    cb[:, CB_SL:CB_SL + 512] = np.tile(sl, (1, 4))
    cb[:, CB_UI:CB_UI + 512] = np.tile(ui, (1, 4))
    cb[:, CB_I4:CB_I4 + 512] = np.tile(eye, (1, 4))
    rm = np.ones(512, np.float32)
    rm[0::128] = 0.0
    cb[:, CB_RM:CB_RM + 512] = rm[None, :]
    ql = np.arange(512)
    for r0 in (0, 64):
        for h in range(4):
            cb[r0 + 0, CB_KAUG + h * 128:CB_KAUG + (h + 1) * 128] = SLOPES[h]
            cb[r0 + 1, CB_KAUG + h * 128:CB_KAUG + (h + 1) * 128] = SLOPES[h]
            cb[r0 + 2, CB_KAUG + h * 128:CB_KAUG + (h + 1) * 128] = SLOPES[h] * np.arange(128)
        cb[r0 + 0, CB_QAUG:CB_QAUG + 512] = -(128.0 + 128.0 * (ql // 128))
        cb[r0 + 1, CB_QAUG:CB_QAUG + 512] = -(ql % 128).astype(np.float32)
        cb[r0 + 2, CB_QAUG:CB_QAUG + 512] = 1.0
    return cst, cb


def host_params(inp):
    g = lambda k: np.asarray(inp[k], np.float32)[0]
    pvec = np.zeros((128, NPV), np.float32)
    pvec[:, PV_ANW:PV_ANW + 8] = _cols(g("attn_norm_w"), 8)
    pvec[:, PV_FNW:PV_FNW + 8] = _cols(g("ffn_norm_w"), 8)
    pvec[:, PV_MU:PV_MU + 14] = _cols(g("mu_shift"), 14)
    pvec[:, PV_W0:PV_W0 + 4] = _cols(g("w0"), 4)
    pvec[:, PV_A0:PV_A0 + 4] = _cols(g("a0"), 4)
    pvec[:, PV_KK:PV_KK + 4] = _cols(g("k_k"), 4)
    pvec[:, PV_KA:PV_KA + 4] = _cols(g("k_a"), 4)
    pvec[:, PV_RK:PV_RK + 4] = _cols(g("r_k").reshape(-1), 4)
    pvec[:, PV_LNW:PV_LNW + 4] = _cols(g("ln_x_w"), 4)
    pvec[:, PV_LNB:PV_LNB + 4] = _cols(g("ln_x_b"), 4)
    pvec[:, PV_QW] = np.tile(g("q_norm_w"), 2)
    pvec[:, PV_KW] = np.tile(g("k_norm_w"), 2)
    cw = g("ffn_conv_w")
    pvec[:, PV_CW0:PV_CW0 + 44] = _cols(cw[0], 44)
    pvec[:, PV_CW1:PV_CW1 + 44] = _cols(cw[1], 44)
    pvec[:, PV_CW2:PV_CW2 + 44] = _cols(cw[2], 44)
    pvec[:, PV_CB:PV_CB + 44] = _cols(g("ffn_conv_b"), 44)
    lamv = np.concatenate([g("lambda_q1"), g("lambda_k1"), g("lambda_q2"), g("lambda_k2")])
    lamv = np.ascontiguousarray(np.broadcast_to(lamv[None, :], (128, 256)))
    subw = np.ascontiguousarray(np.broadcast_to(g("subln_w")[None, :], (128, 128)))
    w_lora = np.ascontiguousarray(np.concatenate([g("w_decay_up"), g("w_aaa_up")], axis=0))
    cst, cb = host_consts()
    return {
        "w_in": np.ascontiguousarray(g("w_in")), "w_out": np.ascontiguousarray(g("w_out")),
        "w_up": np.ascontiguousarray(g("w_ffn_up")), "w_dn": np.ascontiguousarray(g("w_ffn_down")),
        "w_lora": w_lora, "w_gate": np.ascontiguousarray(g("w_gate_up")),
        "pvec": pvec, "cst": cst, "cstb": cb, "lamv": lamv, "subw": subw,
    }


_NC_CACHE = {}


def run(inputs, T, n_cores, **flags):
    key = (T, tuple(sorted(flags.items())))
    if key not in _NC_CACHE:
        _NC_CACHE[key] = build(T, **flags)
    nc = _NC_CACHE[key]
    shared = host_params(inputs)
    xfull = np.asarray(inputs["x"], np.float32)
    in_maps = []
    for b in range(n_cores):
        m = dict(shared)
        m["x"] = np.ascontiguousarray(xfull[b, :T])
        in_maps.append(m)
    res = run_bass_kernel_spmd(nc, in_maps, core_ids=list(range(n_cores)))
    return np.stack([np.asarray(r["out"]) for r in res.results], axis=0)


def kernel(**inputs):
    return run(inputs, 4096, 8).astype(np.float32)
```
